# Optimizing a Trainium2 kernel written in Bass

```python
import math
import jax, jax.numpy as jnp
from jax import lax
import numpy as np

D_MODEL = 1024
BATCH = 4
SEQ = 8192
DEPTH = 1

CHUNK = 64
Q_BLOCK = 128
ATT_WIDTH = D_MODEL // 2
N_DIFF_HEADS = 4
DIFF_HEAD_DIM = ATT_WIDTH // (2 * N_DIFF_HEADS)
GMLP_WIDTH = D_MODEL - ATT_WIDTH
N_GMLP_GROUPS = 4
GMLP_GROUP_DIM = GMLP_WIDTH // N_GMLP_GROUPS
GMLP_CHUNK = 128
IN_WIDTH = 3 * ATT_WIDTH + 2 * GMLP_WIDTH
D_FF = 2816
CONV_WIDTH = 3
ROPE_THETA = 10000.0
LN_EPS = 1e-5
DEEPNORM_ALPHA = (2 * DEPTH) ** 0.25
DEEPNORM_BETA = (8 * DEPTH) ** -0.25

kernel_name = "hybrid_diffattn_gmlp_convffn_deepnorm"


def layer_norm(x, g, b):
    xf = x.astype(jnp.float32)
    mu = jnp.mean(xf, axis=-1, keepdims=True)
    var = jnp.mean(jnp.square(xf - mu), axis=-1, keepdims=True)
    y = (xf - mu) * lax.rsqrt(var + LN_EPS) * g.astype(jnp.float32) + b.astype(jnp.float32)
    return y.astype(x.dtype)


def rms_norm(x, g):
    xf = x.astype(jnp.float32)
    y = xf * lax.rsqrt(jnp.mean(jnp.square(xf), axis=-1, keepdims=True) + LN_EPS) * g.astype(jnp.float32)
    return y.astype(x.dtype)


def rope_tables(seq_len):
    pos = jnp.arange(seq_len, dtype=jnp.float32)
    inv_freq = 1.0 / (ROPE_THETA ** (jnp.arange(0, DIFF_HEAD_DIM, 2, dtype=jnp.float32) / DIFF_HEAD_DIM))
    ang = pos[:, None] * inv_freq[None, :]
    ang = jnp.concatenate([ang, ang], axis=-1)
    return jnp.cos(ang), jnp.sin(ang)


def apply_rope(t, cos, sin):
    half = DIFF_HEAD_DIM // 2
    t1, t2 = t[..., :half], t[..., half:]
    rot = jnp.concatenate([-t2, t1], axis=-1)
    out = t.astype(jnp.float32) * cos[None, :, None, :] + rot.astype(jnp.float32) * sin[None, :, None, :]
    return out.astype(t.dtype)


def diff_attention(q, k, v, lam, cos, sin):
    B, S = q.shape[0], q.shape[1]
    n_blk = S // Q_BLOCK
    q = apply_rope(q, cos, sin) * (DIFF_HEAD_DIM ** -0.5)
    k = apply_rope(k, cos, sin)
    q_blocks = q.reshape(B, n_blk, Q_BLOCK, 2 * N_DIFF_HEADS, DIFF_HEAD_DIM).transpose(1, 0, 3, 2, 4)
    kt = k.transpose(0, 2, 1, 3)
    vt = v.transpose(0, 2, 1, 3)
    k_chunk = jnp.arange(S, dtype=jnp.int32) // CHUNK
    q_chunk = k_chunk.reshape(n_blk, Q_BLOCK)

    def one_block(args):
        q_blk, qc = args
        s = jnp.einsum('bhqd,bhkd->bhqk', q_blk, kt, preferred_element_type=jnp.float32)
        mask = k_chunk[None, :] <= qc[:, None]
        p = jax.nn.softmax(jnp.where(mask, s, -jnp.inf), axis=-1)
        p = p.reshape(B, N_DIFF_HEADS, 2, Q_BLOCK, S)
        a = (p[:, :, 0] - lam * p[:, :, 1]).astype(vt.dtype)
        return jnp.einsum('bhqk,bhkd->bhqd', a, vt)

    o = lax.map(one_block, (q_blocks, q_chunk))
    return o.transpose(1, 0, 3, 2, 4).reshape(B, S, N_DIFF_HEADS, 2 * DIFF_HEAD_DIM)


def spatial_gating(z, ln_g, ln_b, w_s, b_s):
    B, S = z.shape[0], z.shape[1]
    n_c = S // GMLP_CHUNK
    u, vg = z[..., :GMLP_WIDTH], z[..., GMLP_WIDTH:]
    vg = vg.reshape(B, S, N_GMLP_GROUPS, GMLP_GROUP_DIM)
    vg = layer_norm(vg, ln_g.reshape(N_GMLP_GROUPS, GMLP_GROUP_DIM), ln_b.reshape(N_GMLP_GROUPS, GMLP_GROUP_DIM))
    vg = vg.reshape(B, n_c, GMLP_CHUNK, N_GMLP_GROUPS, GMLP_GROUP_DIM)
    w_causal = jnp.tril(w_s)
    gate = jnp.einsum('gts,bcsgd->bctgd', w_causal, vg) + b_s.T[None, None, :, :, None]
    u = u.reshape(B, n_c, GMLP_CHUNK, N_GMLP_GROUPS, GMLP_GROUP_DIM)
    return (u * gate).reshape(B, S, GMLP_WIDTH)


def causal_dwconv(h, w, b):
    S = h.shape[1]
    hp = jnp.pad(h, ((0, 0), (CONV_WIDTH - 1, 0), (0, 0)))
    y = b
    for j in range(CONV_WIDTH):
        y = y + w[j] * hp[:, j:j + S]
    return y


def setup_inputs(seed: int = 0) -> dict:
    key = jax.random.key(seed)
    ks = jax.random.split(key, 24)
    f32 = jnp.float32
    nrm = lambda k, shape, scale: jax.random.normal(k, shape, f32) * scale
    L = DEPTH
    return {
        "x": jax.random.normal(ks[0], (BATCH, SEQ, D_MODEL), f32),
        "w_in": nrm(ks[1], (L, D_MODEL, IN_WIDTH), D_MODEL ** -0.5),
        "lambda_q1": nrm(ks[2], (L, DIFF_HEAD_DIM), 0.1),
        "lambda_k1": nrm(ks[3], (L, DIFF_HEAD_DIM), 0.1),
        "lambda_q2": nrm(ks[4], (L, DIFF_HEAD_DIM), 0.1),
        "lambda_k2": nrm(ks[5], (L, DIFF_HEAD_DIM), 0.1),
        "subln_g": 1.0 + nrm(ks[6], (L, 2 * DIFF_HEAD_DIM), 0.02),
        "gmlp_ln_g": 1.0 + nrm(ks[7], (L, GMLP_WIDTH), 0.02),
        "gmlp_ln_b": nrm(ks[8], (L, GMLP_WIDTH), 0.02),
        "w_spatial": nrm(ks[9], (L, N_GMLP_GROUPS, GMLP_CHUNK, GMLP_CHUNK), GMLP_CHUNK ** -0.5),
        "b_spatial": 1.0 + nrm(ks[10], (L, N_GMLP_GROUPS, GMLP_CHUNK), 0.02),
        "w_out": nrm(ks[11], (L, D_MODEL, D_MODEL), D_MODEL ** -0.5 * DEEPNORM_BETA),
        "ln1_g": 1.0 + nrm(ks[12], (L, D_MODEL), 0.02),
        "ln1_b": nrm(ks[13], (L, D_MODEL), 0.02),
        "w_gate": nrm(ks[14], (L, D_MODEL, D_FF), D_MODEL ** -0.5),
        "w_up": nrm(ks[15], (L, D_MODEL, D_FF), D_MODEL ** -0.5),
        "conv_w": nrm(ks[16], (L, CONV_WIDTH, D_FF), CONV_WIDTH ** -0.5),
        "conv_b": nrm(ks[17], (L, D_FF), 0.02),
        "w_down": nrm(ks[18], (L, D_FF, D_MODEL), D_FF ** -0.5 * DEEPNORM_BETA),
        "ln2_g": 1.0 + nrm(ks[19], (L, D_MODEL), 0.02),
        "ln2_b": nrm(ks[20], (L, D_MODEL), 0.02),
    }


def reference(x, w_in, lambda_q1, lambda_k1, lambda_q2, lambda_k2, subln_g, gmlp_ln_g, gmlp_ln_b,
              w_spatial, b_spatial, w_out, ln1_g, ln1_b, w_gate, w_up, conv_w, conv_b, w_down,
              ln2_g, ln2_b):
    B, S, _ = x.shape
    cos, sin = rope_tables(S)
    for l in range(DEPTH):
        lambda_init = 0.8 - 0.6 * math.exp(-0.3 * l)
        h = jnp.einsum('bsd,de->bse', x, w_in[l])
        qa = h[..., :ATT_WIDTH].reshape(B, S, 2 * N_DIFF_HEADS, DIFF_HEAD_DIM)
        ka = h[..., ATT_WIDTH:2 * ATT_WIDTH].reshape(B, S, 2 * N_DIFF_HEADS, DIFF_HEAD_DIM)
        va = h[..., 2 * ATT_WIDTH:3 * ATT_WIDTH].reshape(B, S, N_DIFF_HEADS, 2 * DIFF_HEAD_DIM)
        zb = h[..., 3 * ATT_WIDTH:]

        lam = (jnp.exp(jnp.sum(lambda_q1[l].astype(jnp.float32) * lambda_k1[l].astype(jnp.float32)))
               - jnp.exp(jnp.sum(lambda_q2[l].astype(jnp.float32) * lambda_k2[l].astype(jnp.float32)))
               + lambda_init)
        oa = diff_attention(qa, ka, va, lam, cos, sin)
        oa = (rms_norm(oa, subln_g[l]) * (1.0 - lambda_init)).reshape(B, S, ATT_WIDTH)

        ob = spatial_gating(jax.nn.gelu(zb, approximate=False), gmlp_ln_g[l], gmlp_ln_b[l],
                            w_spatial[l], b_spatial[l])

        mix = jnp.einsum('bse,ed->bsd', jnp.concatenate([oa, ob], axis=-1), w_out[l])
        x = layer_norm(DEEPNORM_ALPHA * x + mix, ln1_g[l], ln1_b[l])

        g = jnp.einsum('bsd,df->bsf', x, w_gate[l])
        up = jnp.einsum('bsd,df->bsf', x, w_up[l])
        g = causal_dwconv(g, conv_w[l], conv_b[l])
        f = jnp.einsum('bsf,fd->bsd', jax.nn.silu(g) * up, w_down[l])
        x = layer_norm(DEEPNORM_ALPHA * x + f, ln2_g[l], ln2_b[l])
    return x
```

```python
import contextlib
import numpy as np
import concourse.bass as bass
import concourse.mybir as mybir
from concourse.bass_utils import run_bass_kernel_spmd

F32 = mybir.dt.float32
BF16 = mybir.dt.bfloat16
AF = mybir.ActivationFunctionType
ALU = mybir.AluOpType
AX = mybir.AxisListType

D = 1024
S = 8192
NB = 4
DFF = 2816
NFC = DFF // 128
TB = 256
NKB = 33
NQROW = 4096 + 256
ALPHA = 2.0 ** 0.25
LN_EPS = 1e-5
LAMBDA_INIT = 0.2
NEG = -30000.0
ENG_NAMES = ("tensor", "vector", "scalar", "gpsimd", "sync")


class _Op:
    __slots__ = ("eng", "fn", "deps", "is_dma", "sem_key", "signal", "ticket")

    def __init__(self, eng, fn, is_dma=False, sem_key=None):
        self.eng = eng
        self.fn = fn
        self.deps = []
        self.is_dma = is_dma
        self.sem_key = sem_key
        self.signal = is_dma
        self.ticket = None


class Prog:
    def __init__(self, nc):
        self.nc = nc
        self.ops = []
        self.last_writer = {}
        self.readers = {}

    @staticmethod
    def _flat(keys):
        out = []
        for k in keys:
            if isinstance(k, (list, tuple)):
                out.extend(Prog._flat(k))
            else:
                out.append(k)
        return out

    def _add(self, op, reads, writes):
        reads = self._flat(reads)
        writes = self._flat(writes)
        deps = []
        seen = set()

        def push(d):
            if d is not None and id(d) not in seen:
                seen.add(id(d))
                deps.append(d)

        raw = set()
        for k in reads:
            w = self.last_writer.get(k)
            push(w)
            if w is not None:
                raw.add(id(w))
        for k in writes:
            push(self.last_writer.get(k))
            for r in self.readers.get(k, ()):
                push(r)
        for d in deps:
            same = (d.eng == op.eng) and not d.is_dma and not op.is_dma
            if same:
                if op.eng == "tensor":
                    continue
                if id(d) not in raw:
                    continue
            op.deps.append(d)
            d.signal = True
        for k in reads:
            self.readers.setdefault(k, []).append(op)
        for k in writes:
            self.last_writer[k] = op
            self.readers[k] = []
        self.ops.append(op)
        return op

    def op(self, eng, method, reads, writes, *args, **kwargs):
        fn = lambda e: getattr(e, method)(*args, **kwargs)
        return self._add(_Op(eng, fn), list(reads), list(writes))

    def dma(self, eng, out, in_, reads=(), writes=(), sem_key=None):
        o = _Op(eng, lambda e: e.dma_start(out=out, in_=in_), is_dma=True, sem_key=sem_key)
        return self._add(o, list(reads), list(writes))

    def emit(self, final_wait_ops=()):
        nc = self.nc
        final_wait_ops = list(final_wait_ops)
        for e in ENG_NAMES:
            if e == "sync":
                continue
            lasts = [o for o in self.ops if o.eng == e and not o.is_dma]
            if lasts:
                lasts[-1].signal = True
                final_wait_ops.append(lasts[-1])
        eng_count = {e: 0 for e in ENG_NAMES}
        dma_count = {}
        for o in self.ops:
            if o.is_dma:
                dma_count[o.sem_key] = dma_count.get(o.sem_key, 0) + 16
                o.ticket = dma_count[o.sem_key]
            elif o.signal:
                eng_count[o.eng] += 1
                o.ticket = eng_count[o.eng]
        with contextlib.ExitStack() as st:
            sems = {}
            for e in ENG_NAMES:
                sems["e_" + e] = st.enter_context(nc.semaphore("e_" + e))
            for i, k in enumerate(dma_count):
                sems["d_" + k] = st.enter_context(nc.semaphore("d%d_%s" % (i, k)))
            block = st.enter_context(nc.Block())

            def semof(o):
                return sems["d_" + o.sem_key] if o.is_dma else sems["e_" + o.eng]

            def run_engine(engname):
                def body(eng):
                    waited = {}
                    for o in self.ops:
                        if o.eng != engname:
                            continue
                        for d in o.deps:
                            s = semof(d)
                            if waited.get(id(s), 0) >= d.ticket:
                                continue
                            eng.wait_ge(s, d.ticket)
                            waited[id(s)] = d.ticket
                        ins = o.fn(eng)
                        if o.is_dma:
                            ins.then_inc(semof(o), 16)
                        elif o.signal:
                            ins.then_inc(semof(o), 1)
                    if engname == "sync":
                        for o in final_wait_ops:
                            eng.wait_ge(semof(o), o.ticket)
                return body

            block.tensor(run_engine("tensor"))
            block.vector(run_engine("vector"))
            block.scalar(run_engine("scalar"))
            block.gpsimd(run_engine("gpsimd"))
            block.sync(run_engine("sync"))


def _steps():
    steps = []

    def add(kind, blk, col0, tb, nondiag, diag, qrow, seg, orow):
        steps.append(dict(kind=kind, blk=blk, col0=col0, tb=tb, nondiag=nondiag, diag=diag,
                          qrow=qrow, seg=seg, orow=orow))

    add("H", 32, 0, 128, list(range(32, 48)), [64], 4096, 0, None)
    for s in range(8):
        add("B", s, 0, 256, list(range(0, 2 * s)) + list(range(32, 48)), [2 * s, 2 * s + 1], s * 256, 0, s * 256)
    add("H", 32, 128, 128, list(range(0, 16)) + list(range(32, 64)), [65], 4224, 1, None)
    for s in range(8, 16):
        add("B", s, 0, 256, list(range(0, 2 * s)) + list(range(32, 64)), [2 * s, 2 * s + 1], s * 256, 1, s * 256)
    col = 0
    for st in steps:
        st["kb_col0"] = col
        col += len(st["nondiag"])
    return steps, col


STEPS, NKBIAS = _steps()
_NEED_BIAS = None


def _need_bias():
    global _NEED_BIAS
    if _NEED_BIAS is None:
        nb = np.zeros((NKBIAS,), bool)
        for r in range(2):
            ktok, _ = _core_tokens(r)
            nb |= (_kbias_table(ktok) != 0.0)
        _NEED_BIAS = nb
    return _NEED_BIAS


def _core_tokens(r):
    if r == 0:
        seg0 = np.arange(0, 2048)
        seg1 = np.arange(6144, 8192)
        others = np.arange(2048, 6144)
        h0 = np.arange(0, 128)
        h1 = np.arange(6016, 6144)
        flags = (0.0, 1.0)
    else:
        seg0 = np.arange(2048, 4096)
        seg1 = np.arange(4096, 6144)
        others = np.concatenate([np.arange(0, 2048), np.arange(6144, 8192)])
        h0 = np.arange(1920, 2048)
        h1 = np.arange(3968, 4096)
        flags = (1.0, 1.0)
    ktok = np.concatenate([seg0, seg1, others, h0, h1])
    return ktok, flags


def _kbias_table(ktok):
    kb = np.zeros((NKBIAS,), np.float32)
    for st in STEPS:
        q0 = int(ktok[st["diag"][0] * 128])
        for i, t in enumerate(st["nondiag"]):
            p0 = int(ktok[t * 128])
            valid = (p0 + 128) <= q0
            kb[st["kb_col0"] + i] = 0.0 if valid else NEG
        cov = np.zeros(S, np.int32)
        for i, t in enumerate(st["nondiag"]):
            if kb[st["kb_col0"] + i] == 0.0:
                cov[ktok[t * 128:(t + 1) * 128]] += 1
        if not (st["kind"] == "H" and q0 == 0):
            assert np.all(cov[:q0] == 1) and np.all(cov[q0:] == 0), (st["kind"], st["blk"], q0)
    return kb


def build_nc(debug=False, p1_blocks=NKB, n_steps=None, stop_stage=None):
    nc = bass.Bass("TRN2", target_bir_lowering=False)

    def din(name, shape, dt=F32):
        return nc.dram_tensor(name, list(shape), dt, kind="ExternalInput").ap()

    xT_h = din("xT", [NKB, 128, 8 * 256])
    xq_h = din("xq", [NQROW, D])
    rope_h = din("rope", [NKB, 128, 2 * 256])
    kbias_h = din("kbias", [128, NKBIAS])
    lntab_h = din("lntab", [128, 4 * D])
    gmtab_h = din("gmtab", [128, 2 * 512])
    bstab_h = din("bstab", [128, 4 * 128])
    wsT_h = din("wsT", [128, 4 * 128])
    convtab_h = din("convtab", [128, NFC * 4])
    lamtab_h = din("lamtab", [128, 4 * 64])
    subg_h = din("subg", [128, 1])
    flags_h = din("flags", [128, 2])
    wk_h = din("wk", [128, 8 * 1024])
    wq_h = din("wq", [128, 8 * 1024])
    wv_h = din("wv", [128, 8 * 512])
    wvg_h = din("wvg", [128, 8 * 512])
    wu_h = din("wu", [128, 8 * 512])
    wo_h = din("wo", [128, 8 * 1024])
    wgu_h = din("wgu", [NFC, 128, 2 * 8 * 128])
    wdn_h = din("wdn", [NFC, 128, 1024])
    y_h = nc.dram_tensor("y", [4096, D], F32, kind="ExternalOutput").ap()
    dbg_h = {}
    if debug:
        for nm, shp in (("d_x1", [128, D]), ("d_cat", [128, 8 * 256]), ("d_q", [128, 4 * 256]), ("d_halo", [128, NFC * 2])):
            dbg_h[nm] = nc.dram_tensor(nm, shp, F32, kind="ExternalOutput").ap()

    wgub = nc.dram_tensor("wgub", [NFC, 128, 2 * 8 * 128], BF16).ap()
    wdnb = nc.dram_tensor("wdnb", [NFC, 128, 1024], BF16).ap()
    kvh = nc.dram_tensor("kvh", [4, NKB, 128, 512], BF16).ap()

    P = Prog(nc)
    need_bias = _need_bias()

    def sb(name, shape, dt):
        return nc.alloc_sbuf_tensor("s_" + name, list(shape), dt)

    WA = sb("WA", [128, 8, 1024], BF16)
    WV = sb("WV", [128, 8, 512], BF16)
    WU = sb("WU", [128, 8, 512], BF16)
    WO = sb("WO", [128, 8, 1024], BF16)
    lntab = sb("lntab", [128, 4, D], F32)
    gmtab = sb("gmtab", [128, 2, 512], F32)
    bstab = sb("bstab", [128, 4, 128], F32)
    wsT = sb("wsT", [128, 4, 128], BF16)
    convtab = sb("convtab", [128, NFC, 4], F32)
    lamtab = sb("lamtab", [128, 4, 64], F32)
    subg = sb("subg", [128, 1], F32)
    flags = sb("flags", [128, 2], F32)
    kbias = sb("kbias", [128, NKBIAS], F32)
    ident = sb("ident", [128, 128], F32)
    ones_b = sb("ones_b", [128, 128], BF16)
    mkL = sb("mkL", [64, 128], BF16)
    mkRH = sb("mkRH", [64, 256], BF16)
    mkRB = sb("mkRB", [64, 2, 512], BF16)
    smalls = sb("smalls", [128, 16], F32)
    lamprod = sb("lamprod", [128, 2, 64], F32)
    halo = sb("halo", [128, NFC, 2], F32)

    xTbuf = [sb("xTbuf%d" % i, [128, 8, 256], BF16) for i in range(2)]
    xstage = [sb("xstage%d" % i, [128, 1024], F32) for i in range(2)]
    ropebuf = [sb("ropebuf%d" % i, [128, 2, 256], F32) for i in range(2)]
    rtmp = [sb("rtmp%d" % i, [128, 2, 256], F32) for i in range(2)]
    qbdB = sb("qbdB", [128, 4, 512], BF16)
    qbdH = sb("qbdH", [128, 4, 256], BF16)
    NKVB = 3
    kvbuf = [sb("kvbuf%d" % i, [128, 4, 512], BF16) for i in range(NKVB)]
    kvout = kvbuf[0:2]
    NEB = 6
    Ebuf = [sb("Ebuf%d" % i, [128, 512], BF16) for i in range(NEB)]
    NZS = 3
    zsum = [sb("zsum%d" % i, [128, 512], BF16) for i in range(NZS)]
    rz = sb("rz", [128, 512], F32)
    onb = sb("onb", [128, 512], F32)
    oa = sb("oa", [128, 256], F32)
    sqb = sb("sqb", [128, 256], BF16)
    lnt = sb("lnt", [128, 256], F32)
    rstd = sb("rstd", [128, 256], F32)
    catT = sb("catT", [128, 8, 256], BF16)
    uT = sb("uT", [128, 4, 256], F32)
    vg = [sb("vg%d" % i, [128, 512], F32) for i in range(2)]
    vgn = sb("vgn", [128, 512], F32)
    vgnb = [sb("vgnb%d" % i, [128, 512], BF16) for i in range(2)]
    gst = sb("gst", [128, 4, 6], F32)
    gmv = sb("gmv", [128, 4, 2], F32)
    glnv = sb("glnv", [128, 4], F32)
    grs = sb("grs", [128, 4], F32)
    gtmp = [sb("gtmp%d" % i, [128, 128], F32) for i in range(2)]
    xres = [sb("xres%d" % i, [128, D], F32) for i in range(2)]
    x1 = [sb("x1_0", [128, 2, D], F32)] * 2
    x1T = sb("x1T", [128, 8, 256], BF16)
    lst = sb("lst", [128, 2, 6], F32)
    lmv = sb("lmv", [128, 2], F32)
    llnv = sb("llnv", [128, 1], F32)
    lrs = sb("lrs", [128, 1], F32)
    gbuf = [sb("gbuf%d" % i, [128, 258], F32) for i in range(2)]
    cbuf = [sb("cbuf%d" % i, [128, 256], F32) for i in range(2)]
    sbuf_ = [sb("sbuf%d" % i, [128, 256], F32) for i in range(2)]
    NHT = 4
    hT = [sb("hT%d" % i, [128, 256], BF16) for i in range(NHT)]
    NWG = 4
    wgubuf = [sb("wgubuf%d" % i, [128, 2, 8, 128], BF16) for i in range(NWG)]
    wdnbuf = [sb("wdnbuf%d" % i, [128, 1024], BF16) for i in range(NWG)]
    dbgbuf = sb("dbgbuf", [128, 8 * 256], F32) if debug else None

    pp = [nc.alloc_psum_tensor("pp%d" % i, [128, 1024], F32) for i in range(4)]
    ps = [pp[i // 2][:, (i % 2) * 512:(i % 2 + 1) * 512] for i in range(8)]

    def psk(i):
        return "ps%d" % i

    class Rot:
        def __init__(self, items):
            self.items = list(items)
            self.i = 0

        def next(self):
            v = self.items[self.i % len(self.items)]
            self.i += 1
            return v

    rotS = Rot([0, 1, 2, 3])
    rotV = Rot([4, 5, 6, 7])

    def mm(out, lhsT, rhs, start, stop, reads, writes, **kw):
        P.op("tensor", "matmul", reads, writes, out, lhsT=lhsT, rhs=rhs, start=start, stop=stop, **kw)

    def act(out, in_, func, reads, writes, **kw):
        P.op("scalar", "activation", reads, writes, out=out, in_=in_, func=func, **kw)

    def tt(eng, out, in0, in1, op, reads, writes):
        P.op(eng, "tensor_tensor", reads, writes, out=out, in0=in0, in1=in1, op=op)

    def ts2(eng, out, in0, s1, s2, op0, op1, reads, writes):
        P.op(eng, "tensor_scalar", reads, writes, out=out, in0=in0, scalar1=s1, scalar2=s2, op0=op0, op1=op1)

    def stt(eng, out, in0, scalar, in1, op0, op1, reads, writes):
        P.op(eng, "scalar_tensor_tensor", reads, writes, out=out, in0=in0, scalar=scalar, in1=in1, op0=op0, op1=op1)

    def memset(ap, val, reads, writes):
        P.op("gpsimd", "memset", reads, writes, ap, val)

    cnt = dict(e=0, kv=0, wg=0, ht=0, gb=0, xr=0, vg=0, gt=0, oz=0, s=0, zs=0, xs=0)

    def nxt(k, mod):
        v = cnt[k] % mod
        cnt[k] += 1
        return v

    eps_c = smalls[:, 0:1]
    memset(smalls[:], 0.0, [], ["smalls"])
    memset(smalls[:, 0:1], LN_EPS, ["smalls"], ["smalls"])
    memset(ident[:], 0.0, [], ["ident"])
    P.op("gpsimd", "affine_select", ["ident"], ["ident"], out=ident[:], in_=ident[:], pattern=[[-1, 128]],
         compare_op=ALU.not_equal, fill=1.0, base=0, channel_multiplier=1)
    memset(ones_b[:], 1.0, [], ["ones_b"])
    memset(mkL[:], 0.0, [], ["mk"])
    memset(mkL[0:1, :], 1.0, ["mk"], ["mk"])
    memset(mkL[32:33, 64:128], 1.0, ["mk"], ["mk"])
    memset(mkRH[:], 0.0, ["mk"], ["mk"])
    memset(mkRB[:], 0.0, ["mk"], ["mk"])
    for a in range(2):
        memset(mkRH[32:33, a * 128:a * 128 + 64], NEG, ["mk"], ["mk"])
        memset(mkRB[32:33, 0, a * 256:a * 256 + 64], NEG, ["mk"], ["mk"])
        memset(mkRB[0:1, 1, a * 256:a * 256 + 128], NEG, ["mk"], ["mk"])
        memset(mkRB[32:33, 1, a * 256 + 128:a * 256 + 192], NEG, ["mk"], ["mk"])
    memset(qbdB[:], 0.0, [], ["qbd"])
    memset(qbdH[:], 0.0, [], ["qbd"])
    memset(halo[:], 0.0, [], ["halo"])

    def convert_ffn_piece(pc):
        a_, b_ = 2 * pc, 2 * pc + 2
        P.dma("gpsimd", wgub[a_:b_], wgu_h[a_:b_], writes=["wgub%d" % pc], sem_key="wgub%d" % pc)
        P.dma("gpsimd", wdnb[a_:b_], wdn_h[a_:b_], writes=["wdnb%d" % pc], sem_key="wdnb%d" % pc)

    def load_cast(dst, dst_key, src, ncols, engines):
        for ci, c in enumerate(range(0, ncols, 1024)):
            xs_ = nxt("xs", 2)
            sk_ = "xstage%d" % xs_
            P.dma("sync", xstage[xs_][:], src[:, c:c + 1024], writes=[sk_], sem_key=sk_)
            eng = engines[ci % len(engines)]
            if eng == "scalar":
                act(dst[:, c:c + 1024], xstage[xs_][:], AF.Copy, [sk_, dst_key], [dst_key])
            else:
                P.op(eng, "tensor_copy", [sk_, dst_key], [dst_key], out=dst[:, c:c + 1024], in_=xstage[xs_][:])

    load_cast(WA[:].rearrange("p k c -> p (k c)"), "WA", wk_h, 8192, ("scalar", "vector"))
    load_cast(WV[:].rearrange("p k c -> p (k c)"), "WV", wv_h, 4096, ("scalar", "vector"))

    def ld(dst, src, key):
        P.dma("sync", dst, src, writes=[key], sem_key=key)

    ld(kbias[:], kbias_h, "kbias")
    ld(lamtab[:].rearrange("p a b -> p (a b)"), lamtab_h, "lamtab")
    ld(subg[:], subg_h, "subg")
    ld(flags[:], flags_h, "flags")
    ld(convtab[:].rearrange("p a b -> p (a b)"), convtab_h, "convtab")
    wsT32 = vgn[:].rearrange("p (a b) -> p a b", a=4)
    ld(vgn[:], wsT_h, "vgn")
    ld(bstab[:].rearrange("p a b -> p (a b)"), bstab_h, "bstab")
    ld(gmtab[:].rearrange("p a b -> p (a b)"), gmtab_h, "gmtab")
    ld(lntab[:].rearrange("p a b -> p (a b)"), lntab_h, "lntab")

    P.op("gpsimd", "affine_select", ["vgn"], ["vgn"], out=wsT32, in_=wsT32, pattern=[[0, 4], [1, 128]],
         compare_op=ALU.is_ge, fill=0.0, base=0, channel_multiplier=-1)
    P.op("vector", "tensor_copy", ["vgn"], ["wsT"], out=wsT[:], in_=wsT32)

    tt("vector", lamprod[:, 0, :], lamtab[:, 0, :], lamtab[:, 1, :], ALU.mult, ["lamtab"], ["lamprod"])
    tt("vector", lamprod[:, 1, :], lamtab[:, 2, :], lamtab[:, 3, :], ALU.mult, ["lamtab", "lamprod"], ["lamprod"])
    P.op("vector", "reduce_sum", ["lamprod", "smalls"], ["smalls"], out=smalls[:, 1:2], in_=lamprod[:, 0, :], axis=AX.X)
    P.op("vector", "reduce_sum", ["lamprod", "smalls"], ["smalls"], out=smalls[:, 2:3], in_=lamprod[:, 1, :], axis=AX.X)
    act(smalls[:, 3:5], smalls[:, 1:3], AF.Exp, ["smalls"], ["smalls"])
    tt("vector", smalls[:, 5:6], smalls[:, 4:5], smalls[:, 3:4], ALU.subtract, ["smalls"], ["smalls"])
    ts2("vector", smalls[:, 6:7], smalls[:, 5:6], -LAMBDA_INIT, 1.0, ALU.add, ALU.mult, ["smalls"], ["smalls"])
    ts2("vector", smalls[:, 7:8], subg[:], 1.0 - LAMBDA_INIT, 0.0, ALU.mult, ALU.add, ["smalls", "subg"], ["smalls"])
    neg_lam = smalls[:, 6:7]
    g08 = smalls[:, 7:8]

    def load_xblock(kb, buf, engines=("scalar",)):
        load_cast(xTbuf[buf][:].rearrange("p k t -> p (k t)"), "xTbuf%d" % buf, xT_h[kb], 2048, engines)
        P.dma("sync", ropebuf[buf][:].rearrange("p a t -> p (a t)"), rope_h[kb],
              writes=["ropebuf%d" % buf], sem_key="ropebuf%d" % buf)

    rt_i = [0]

    def proj_rope(xt, xkey, rp, rpkey, c0, n, cc, outs, out_key):
        bank = rotS.next()
        for half in range(2):
            for kc in range(8):
                mm(ps[bank][:, half * 256:half * 256 + n],
                   WA[:, kc, half * 512 + cc * 128:half * 512 + (cc + 1) * 128],
                   xt[:, kc, c0:c0 + n], kc == 0, kc == 7, ["WA", xkey], [psk(bank)])
        ti = rt_i[0] % 2
        rt_i[0] += 1
        tk = "rtmp%d" % ti
        tt("vector", rtmp[ti][:, 0, 0:n], ps[bank][:, 0:n], rp[:, 0, c0:c0 + n], ALU.mult, [psk(bank), rpkey], [tk])
        tt("vector", rtmp[ti][:, 1, 0:n], ps[bank][:, 256:256 + n], rp[:, 1, c0:c0 + n], ALU.mult,
           [psk(bank), rpkey, tk], [tk])
        for (r0, r1, out_ap) in outs:
            tt("gpsimd", out_ap, rtmp[ti][r0:r1, 0, 0:n], rtmp[ti][r0:r1, 1, 0:n], ALU.add, [tk, out_key], [out_key])

    load_xblock(0, 0)
    for kb in range(p1_blocks):
        buf = kb % 2
        if kb + 1 < p1_blocks:
            load_xblock(kb + 1, (kb + 1) % 2)
        if 1 <= kb <= NFC // 2:
            convert_ffn_piece(kb - 1)
        if kb == 2:
            load_cast(WU[:].rearrange("p k c -> p (k c)"), "WU", wu_h, 4096, ("scalar",))
        if kb == 4:
            load_cast(WO[:].rearrange("p k c -> p (k c)"), "WO", wo_h, 8192, ("scalar",))
        xt = xTbuf[buf]
        rp = ropebuf[buf]
        ko = kvout[buf]
        kok = "kvbuf%d" % buf
        for cc in range(4):
            proj_rope(xt, "xTbuf%d" % buf, rp, "ropebuf%d" % buf, 0, 256, cc, [(0, 128, ko[:, cc, 0:256])], kok)
        for t in range(2):
            bank = rotV.next()
            for kc in range(8):
                mm(ps[bank][:, 0:512], xt[:, kc, t * 128:(t + 1) * 128], WV[:, kc, :], kc == 0, kc == 7,
                   ["WV", "xTbuf%d" % buf], [psk(bank)])
            act(ko[:, :, 256 + t * 128:256 + (t + 1) * 128], ps[bank].rearrange("p (h d) -> p h d", h=4), AF.Copy,
                [psk(bank), kok], [kok])
        P.dma("sync", kvh[:, kb].rearrange("h p c -> p h c"), ko[:], reads=[kok], writes=["kvh%d" % kb], sem_key=kok)

    load_cast(WA[:].rearrange("p k c -> p (k c)"), "WA", wq_h, 8192, ("scalar", "vector"))
    load_cast(WV[:].rearrange("p k c -> p (k c)"), "WV", wvg_h, 4096, ("scalar", "vector"))

    final_ops = []

    def group_blocks(blocks):
        groups = []
        cur = []
        for b in blocks:
            if cur and (b != cur[-1] + 1 or len(cur) == 4):
                groups.append(cur)
                cur = []
            cur.append(b)
        if cur:
            groups.append(cur)
        return groups

    def emit_ln(xb_, xk_, t, gi, bi_):
        for hf in range(2):
            P.op("vector", "bn_stats", [xk_, "lst"], ["lst"], out=lst[:, hf, :], in_=xb_[:, t, hf * 512:(hf + 1) * 512])
        P.op("vector", "bn_aggr", ["lst"], ["lmv"], out=lmv[:], in_=lst[:].rearrange("p a b -> p (a b)"))
        act(llnv[:], lmv[:, 1:2], AF.Ln, ["lmv", "smalls"], ["llnv"], bias=eps_c, scale=1.0)
        act(lrs[:], llnv[:], AF.Exp, ["llnv"], ["lrs"], scale=-0.5)
        stt("vector", xb_[:, t, :], xb_[:, t, :], lmv[:, 0:1], lntab[:, gi, :], ALU.subtract, ALU.mult,
            [xk_, "lmv", "lntab"], [xk_])
        stt("vector", xb_[:, t, :], xb_[:, t, :], lrs[:, 0:1], lntab[:, bi_, :], ALU.mult, ALU.add,
            [xk_, "lrs", "lntab"], [xk_])

    def make_kv_stream(st):
        nd_blocks = sorted(set(t // 2 for t in st["nondiag"]))
        vis = {}
        for i, t in enumerate(st["nondiag"]):
            vis.setdefault(t // 2, []).append((t, False, 0, st["kb_col0"] + i))
        dblk = st["diag"][0] // 2
        dvis = [(t, True, di, None) for di, t in enumerate(st["diag"])]
        GL = []
        per_head = []
        for h in range(4):
            tiles = []
            for g in group_blocks(nd_blocks):
                kvb = nxt("kv", NKVB)
                GL.append((h, g, kvb))
                for bi, b in enumerate(g):
                    for v in vis[b]:
                        tiles.append((v, kvb, bi, len(GL) - 1))
            kvb = nxt("kv", NKVB)
            GL.append((h, [dblk], kvb))
            for v in dvis:
                tiles.append((v, kvb, 0, len(GL) - 1))
            per_head.append(tiles)
        glast = {}
        for h in range(4):
            for ti_, tl in enumerate(per_head[h]):
                glast[tl[3]] = (h, ti_)
        issued = [0]

        def issue(upto):
            while issued[0] < min(upto, len(GL)):
                h_, g, kvb_ = GL[issued[0]]
                P.dma("sync", kvbuf[kvb_][:, 0:len(g), :], kvh[h_, g[0]:g[0] + len(g)].rearrange("b p c -> p b c"),
                      reads=["kvh%d" % b for b in g], writes=["kvbuf%d" % kvb_], sem_key="kvbuf%d" % kvb_)
                issued[0] += 1

        def done_upto(h, local_idx):
            gd = -1
            for gi_ in range(len(GL)):
                hh, tt_ = glast[gi_]
                if hh < h or (hh == h and tt_ <= local_idx):
                    gd = gi_
                else:
                    break
            return gd

        return dict(per_head=per_head, issue=issue, done_upto=done_upto)

    def attention_head(st, h, tb, isH, kvs, deferred):
        ozs = nxt("oz", 2)
        bO = 4 + 2 * ozs
        bZ = 5 + 2 * ozs
        n2 = 2 * tb
        qbd = qbdH if isH else qbdB
        tiles = kvs["per_head"][h]
        ntile = len(tiles)
        info = []

        def emit_S(idx):
            (t, isd, di, bcol), kvb_, bi, gidx = tiles[idx]
            half = t % 2
            bank = rotS.next()
            ebi = nxt("e", NEB)
            mm(ps[bank][:, 0:n2], kvbuf[kvb_][:, bi, half * 128:(half + 1) * 128], qbd[:, h, 0:n2], True, not isd,
               ["kvbuf%d" % kvb_, "qbd"], [psk(bank)])
            if isd:
                mrhs = mkRH[0:33, 0:n2] if isH else mkRB[0:33, di, 0:n2]
                mm(ps[bank][:, 0:n2], mkL[0:33, :], mrhs, False, True, ["mk"], [psk(bank)])
                act(Ebuf[ebi][:, 0:n2], ps[bank][:, 0:n2], AF.Exp, [psk(bank)], ["Ebuf%d" % ebi], scale=0.125)
            elif need_bias[bcol]:
                act(Ebuf[ebi][:, 0:n2], ps[bank][:, 0:n2], AF.Exp, [psk(bank), "kbias"], ["Ebuf%d" % ebi],
                    bias=kbias[:, bcol:bcol + 1], scale=0.125)
            else:
                act(Ebuf[ebi][:, 0:n2], ps[bank][:, 0:n2], AF.Exp, [psk(bank)], ["Ebuf%d" % ebi], scale=0.125)
            info.append(ebi)

        ZG = 8
        zstate = dict(first=True)

        def emit_PV(idx):
            (t, isd, di, bcol), kvb_, bi, gidx = tiles[idx]
            half = t % 2
            ebi = info[idx]
            first = idx == 0
            last = idx == ntile - 1
            mm(ps[bO][:, 0:n2], kvbuf[kvb_][:, bi, 256 + half * 128:256 + (half + 1) * 128], Ebuf[ebi][:, 0:n2],
               first, last, ["kvbuf%d" % kvb_, "Ebuf%d" % ebi], [psk(bO)])
            if idx % ZG == 0:
                zstate["cur"], zstate["ckey"] = Ebuf[ebi], "Ebuf%d" % ebi
            else:
                zi = nxt("zs", NZS)
                tt("vector", zsum[zi][:, 0:n2], zstate["cur"][:, 0:n2], Ebuf[ebi][:, 0:n2], ALU.add,
                   [zstate["ckey"], "Ebuf%d" % ebi], ["zsum%d" % zi])
                zstate["cur"], zstate["ckey"] = zsum[zi], "zsum%d" % zi
            if (idx % ZG == ZG - 1) or last:
                mm(ps[bZ][:, 0:n2], ones_b[:], zstate["cur"][:, 0:n2], zstate["first"], last,
                   ["ones_b", zstate["ckey"]], [psk(bZ)])
                zstate["first"] = False

        LOOK = 3
        if h == 0:
            kvs["issue"](NKVB)
        for idx in range(ntile + LOOK):
            if idx < ntile:
                emit_S(idx)
            if idx - LOOK >= 0:
                emit_PV(idx - LOOK)
                kvs["issue"](kvs["done_upto"](h, idx - LOOK) + 1 + NKVB)
            if idx == 6 and deferred is not None:
                deferred()
                deferred = None
        if deferred is not None:
            deferred()

        def epilogue():
          P.op("vector", "reciprocal", [psk(bZ)], ["rz"], out=rz[:, 0:n2], in_=ps[bZ][:, 0:n2])
          tt("vector", onb[:, 0:n2], ps[bO][:, 0:n2], rz[:, 0:n2], ALU.mult, [psk(bO), "rz"], ["onb"])
          stt("vector", oa[:, 0:tb], onb[:, tb:n2], neg_lam, onb[:, 0:tb], ALU.mult, ALU.add, ["onb", "smalls"], ["oa"])
          tt("gpsimd", sqb[:, 0:tb], oa[:, 0:tb], oa[:, 0:tb], ALU.mult, ["oa"], ["sqb"])
          mm(ps[bZ][:, 0:tb], ones_b[:], sqb[:, 0:tb], True, True, ["ones_b", "sqb"], [psk(bZ)])
          act(lnt[:, 0:tb], ps[bZ][:, 0:tb], AF.Ln, [psk(bZ), "smalls"], ["lnt"], bias=eps_c, scale=1.0 / 128.0)
          act(rstd[:, 0:tb], lnt[:, 0:tb], AF.Exp, ["lnt"], ["rstd"], scale=-0.5)
          stt("vector", catT[:, h, 0:tb], oa[:, 0:tb], g08, rstd[:, 0:tb], ALU.mult, ALU.mult,
              ["oa", "rstd", "smalls", "catT"], ["catT"])

        return epilogue

    def emit_front(si):
        st = step_list[si]
        tb = st["tb"]
        nt = tb // 128
        isH = st["kind"] == "H"
        c0q = st["col0"]
        xb_i = si % 2
        x1b = x1[si % 2]
        x1k = "x1_0"
        xkey = "xTbuf%d" % xb_i
        rpkey = "ropebuf%d" % xb_i
        xt = xTbuf[xb_i]
        rp = ropebuf[xb_i]
        if si == 0:
            load_xblock(st["blk"], xb_i)
        qbd = qbdH if isH else qbdB
        xis = []
        for t in range(nt):
            xi = nxt("xr", 2)
            xis.append(xi)
            r0 = st["qrow"] + t * 128
            P.dma("sync", xres[xi][:], xq_h[r0:r0 + 128, :], writes=["xres%d" % xi], sem_key="xres%d" % xi)

        for cc in range(4):
            proj_rope(xt, xkey, rp, rpkey, c0q, tb, cc,
                      [(0, 64, qbd[0:64, cc, 0:tb]), (64, 128, qbd[64:128, cc, tb:2 * tb])], "qbd")

        for g in range(4):
            bank = rotV.next()
            for kc in range(8):
                mm(ps[bank][:, 0:tb], WU[:, kc, g * 128:(g + 1) * 128], xt[:, kc, c0q:c0q + tb], kc == 0, kc == 7,
                   ["WU", xkey], [psk(bank)])
            act(uT[:, g, 0:tb], ps[bank][:, 0:tb], AF.Gelu, [psk(bank), "uT"], ["uT"])
        vis_ = []
        for t in range(nt):
            bank = rotV.next()
            vi = nxt("vg", 2)
            vis_.append(vi)
            vk = "vg%d" % vi
            for kc in range(8):
                mm(ps[bank][:, 0:512], xt[:, kc, c0q + t * 128:c0q + (t + 1) * 128], WV[:, kc, :], kc == 0, kc == 7,
                   ["WV", xkey], [psk(bank)])
            act(vg[vi][:], ps[bank], AF.Gelu, [psk(bank)], [vk])
        if si + 1 < len(step_list):
            load_xblock(step_list[si + 1]["blk"], (si + 1) % 2, engines=("vector",))
        def gmlp_ln():
          for t in range(nt):
            vi = vis_[t]
            vk = "vg%d" % vi
            for g in range(4):
                P.op("vector", "bn_stats", [vk, "gst"], ["gst"], out=gst[:, g, :], in_=vg[vi][:, g * 128:(g + 1) * 128])
            for g in range(4):
                P.op("vector", "bn_aggr", ["gst", "gmv"], ["gmv"], out=gmv[:, g, :], in_=gst[:, g, :])
            act(glnv[:], gmv[:, :, 1], AF.Ln, ["gmv", "smalls"], ["glnv"], bias=eps_c, scale=1.0)
            act(grs[:], glnv[:], AF.Exp, ["glnv"], ["grs"], scale=-0.5)
            for g in range(4):
                gs = slice(g * 128, (g + 1) * 128)
                stt("vector", vgn[:, gs], vg[vi][:, gs], gmv[:, g, 0:1], gmtab[:, 0, gs], ALU.subtract, ALU.mult,
                    [vk, "gmv", "gmtab", "vgn"], ["vgn"])
                stt("vector", vgnb[vi][:, gs], vgn[:, gs], grs[:, g:g + 1], gmtab[:, 1, gs], ALU.mult, ALU.add,
                    ["vgn", "grs", "gmtab", "vgnb%d" % vi], ["vgnb%d" % vi])

        def gmlp_part2():
            for t in range(nt):
                vi = vis_[t]
                gbank = rotS.next()
                for g in range(4):
                    mm(ps[gbank][:, g * 128:(g + 1) * 128], vgnb[vi][:, g * 128:(g + 1) * 128], wsT[:, g, :], True, True,
                       ["vgnb%d" % vi, "wsT"], [psk(gbank)])
                for g in range(4):
                    gi = nxt("gt", 2)
                    tt("vector", gtmp[gi][:, 0:128], ps[gbank][:, g * 128:(g + 1) * 128], bstab[:, g, :], ALU.add,
                       [psk(gbank), "bstab"], ["gtmp%d" % gi])
                    tt("gpsimd", catT[:, 4 + g, t * 128:(t + 1) * 128], gtmp[gi][:, 0:128], uT[:, g, t * 128:(t + 1) * 128],
                       ALU.mult, ["gtmp%d" % gi, "uT", "catT"], ["catT"])

        return dict(gmlp_part2=gmlp_part2, xis=xis, gmlp_ln=gmlp_ln)

    step_list = STEPS if n_steps is None else STEPS[:n_steps]
    fctx = {}
    if step_list:
        fctx[0] = emit_front(0)
        fctx[0]["gmlp_ln"]()
    for si, st in enumerate(step_list):
        tb = st["tb"]
        nt = tb // 128
        isH = st["kind"] == "H"
        x1b = x1[0]
        x1k = "x1_0"
        gmlp_part2 = fctx[si]["gmlp_part2"]
        xis = fctx[si]["xis"]
        kvs = make_kv_stream(st)
        pend = None
        for h in range(4):
            if h == 1:
                ep0 = pend

                def pend(ep0=ep0):
                    ep0()
                    gmlp_part2()
            pend = attention_head(st, h, tb, isH, kvs, pend)
        pend()

        bank_of = []
        for t in range(nt):
            banks = (rotV.next(), rotV.next())
            bank_of.append(banks)
            for hf in range(2):
                for ec in range(8):
                    mm(ps[banks[hf]][:, 0:512], catT[:, ec, t * 128:(t + 1) * 128], WO[:, ec, hf * 512:(hf + 1) * 512],
                       ec == 0, ec == 7, ["catT", "WO"], [psk(banks[hf])])
        for t in range(nt):
            xi = xis[t]
            banks = bank_of[t]
            for hf in range(2):
                stt("vector", x1b[:, t, hf * 512:(hf + 1) * 512], xres[xi][:, hf * 512:(hf + 1) * 512], ALPHA,
                    ps[banks[hf]][:, 0:512], ALU.mult, ALU.add, ["xres%d" % xi, psk(banks[hf]), x1k], [x1k])
            emit_ln(x1b, x1k, t, 0, 1)
        if si + 1 < len(step_list):
            fctx[si + 1] = emit_front(si + 1)
        for t in range(nt):
            for half4 in range(2):
                bank = rotS.next()
                for j in range(4):
                    kc = half4 * 4 + j
                    P.op("tensor", "transpose", [x1k, "ident"], [psk(bank)], out=ps[bank][:, j * 128:(j + 1) * 128],
                         in_=x1b[:, t, kc * 128:(kc + 1) * 128], identity=ident[:])
                act(x1T[:, half4 * 4:(half4 + 1) * 4, t * 128:(t + 1) * 128],
                    ps[bank].rearrange("p (j c) -> p j c", j=4), AF.Copy, [psk(bank), "x1T"], ["x1T"])
        if si + 1 < len(step_list):
            fctx[si + 1]["gmlp_ln"]()

        if debug and si == 1:
            o = P.dma("sync", dbg_h["d_x1"], x1b[:, 0, :], reads=[x1k], writes=["d_x1"], sem_key="d_x1")
            final_ops.append(o)
            P.op("vector", "tensor_copy", ["catT"], ["dbgbuf"], out=dbgbuf[:], in_=catT[:].rearrange("p a b -> p (a b)"))
            o = P.dma("sync", dbg_h["d_cat"], dbgbuf[:], reads=["dbgbuf"], writes=["d_cat"], sem_key="d_cat")
            final_ops.append(o)
            o = P.dma("sync", dbg_h["d_halo"], halo[:].rearrange("p a b -> p (a b)"), reads=["halo"], writes=["d_halo"], sem_key="d_halo")
            final_ops.append(o)

        if isH:
            hb = rotV.next()
            for fc in range(NFC):
                wi = nxt("wg", NWG)
                P.dma("sync", wgubuf[wi][:, 0, :, :].rearrange("p k f -> p (k f)"), wgub[fc][:, 0:1024], reads=["wgub%d" % (fc // 2)],
                      writes=["wgubuf%d" % wi], sem_key="wgubuf%d" % wi)
                for kc in range(8):
                    mm(ps[hb][:, 2 * fc:2 * fc + 2], wgubuf[wi][:, 0, kc, :], x1T[:, kc, 126:128], kc == 0, kc == 7,
                       ["wgubuf%d" % wi, "x1T"], [psk(hb)])
            seg = st["seg"]
            ts2("vector", halo[:].rearrange("p a b -> p (a b)"), ps[hb][:, 0:2 * NFC], flags[:, seg:seg + 1], 0.0,
                ALU.mult, ALU.add, [psk(hb), "flags", "halo"], ["halo"])
            continue

        accb = (4, 5, 6, 7)
        pending = []

        def emit_down(fc, hi, wi):
            for t in range(2):
                for hf in range(2):
                    b = accb[t * 2 + hf]
                    mm(ps[b][:, 0:512], hT[hi][:, t * 128:(t + 1) * 128], wdnbuf[wi][:, hf * 512:(hf + 1) * 512],
                       fc == 0, fc == NFC - 1, ["hT%d" % hi, "wdnbuf%d" % wi], [psk(b)], skip_group_check=True)

        for fc in range(NFC):
            wi = nxt("wg", NWG)
            P.dma("sync", wgubuf[wi][:].rearrange("p a k f -> p (a k f)"), wgub[fc], reads=["wgub%d" % (fc // 2)],
                  writes=["wgubuf%d" % wi], sem_key="wgubuf%d" % wi)
            P.dma("sync", wdnbuf[wi][:], wdnb[fc], reads=["wdnb%d" % (fc // 2)], writes=["wdnbuf%d" % wi], sem_key="wdnbuf%d" % wi)
            bank = rotS.next()
            for a in range(2):
                for kc in range(8):
                    mm(ps[bank][:, a * 256:(a + 1) * 256], wgubuf[wi][:, a, kc, :], x1T[:, kc, :], kc == 0, kc == 7,
                       ["wgubuf%d" % wi, "x1T"], [psk(bank)])
            if len(pending) >= 2:
                emit_down(*pending.pop(0))
            gi = nxt("gb", 2)
            gk = "gbuf%d" % gi
            ck = "cbuf%d" % gi
            sk = "sbuf%d" % gi
            P.op("gpsimd", "tensor_copy", ["halo", gk], [gk], out=gbuf[gi][:, 0:2], in_=halo[:, fc, :])
            act(gbuf[gi][:, 2:258], ps[bank][:, 0:256], AF.Copy, [psk(bank), gk], [gk])
            P.op("gpsimd", "tensor_copy", [gk, "halo"], ["halo"], out=halo[:, fc, :], in_=gbuf[gi][:, 256:258])
            ts2("vector", cbuf[gi][:], gbuf[gi][:, 2:258], convtab[:, fc, 2:3], convtab[:, fc, 3:4], ALU.mult, ALU.add,
                [gk, "convtab"], [ck])
            stt("vector", cbuf[gi][:], gbuf[gi][:, 1:257], convtab[:, fc, 1:2], cbuf[gi][:], ALU.mult, ALU.add,
                [gk, ck, "convtab"], [ck])
            stt("vector", cbuf[gi][:], gbuf[gi][:, 0:256], convtab[:, fc, 0:1], cbuf[gi][:], ALU.mult, ALU.add,
                [gk, ck, "convtab"], [ck])
            act(sbuf_[gi][:], cbuf[gi][:], AF.Silu, [ck], [sk])
            hi = nxt("ht", NHT)
            tt("vector", hT[hi][:], sbuf_[gi][:], ps[bank][:, 256:512], ALU.mult, [sk, psk(bank)], ["hT%d" % hi])
            pending.append((fc, hi, wi))
        while pending:
            emit_down(*pending.pop(0))

        for t in range(2):
            for hf in range(2):
                b = accb[t * 2 + hf]
                stt("vector", x1b[:, t, hf * 512:(hf + 1) * 512], x1b[:, t, hf * 512:(hf + 1) * 512], ALPHA,
                    ps[b][:, 0:512], ALU.mult, ALU.add, [x1k, psk(b)], [x1k])
            emit_ln(x1b, x1k, t, 2, 3)
        o = P.dma("sync", y_h[st["orow"]:st["orow"] + 256, :].rearrange("(t p) d -> p t d", p=128), x1b[:],
                  reads=[x1k], writes=["y%d" % si], sem_key=x1k)
        final_ops.append(o)

    P.emit(final_wait_ops=final_ops)
    return nc


def _rope_tables(ktok):
    pos = ktok.astype(np.float32)
    inv_freq = (1.0 / (np.float32(10000.0) ** (np.arange(0, 64, 2, dtype=np.float32) / np.float32(64.0)))).astype(np.float32)
    ang = (pos[:, None] * inv_freq[None, :]).astype(np.float32)
    cos = np.cos(ang).astype(np.float32)
    sin = np.sin(ang).astype(np.float32)
    j = np.arange(128) % 64
    f = j % 32
    sign = np.where(j < 32, -1.0, 1.0).astype(np.float32)
    cosT = cos[:, f].T
    sinT = (sin[:, f] * sign[None, :]).T
    T = ktok.shape[0]
    tab = np.stack([cosT.reshape(128, T // 256, 256), sinT.reshape(128, T // 256, 256)], axis=2)
    return np.ascontiguousarray(tab.transpose(1, 0, 2, 3)).reshape(T // 256, 128, 512).astype(np.float32)


def _chunk_rows(w):
    K, N = w.shape
    return np.ascontiguousarray(w.reshape(K // 128, 128, N).transpose(1, 0, 2))


def _bcast(v):
    return np.ascontiguousarray(np.broadcast_to(np.asarray(v, np.float32).reshape(1, -1), (128, np.asarray(v).size)))


_NC_CACHE = {}


def kernel(x, w_in, lambda_q1, lambda_k1, lambda_q2, lambda_k2, subln_g, gmlp_ln_g, gmlp_ln_b,
           w_spatial, b_spatial, w_out, ln1_g, ln1_b, w_gate, w_up, conv_w, conv_b, w_down,
           ln2_g, ln2_b):
    f = lambda a: np.asarray(a, dtype=np.float32)
    x = f(x)
    w_in = f(w_in)[0]
    w_out = f(w_out)[0]
    w_gate = f(w_gate)[0]
    w_up = f(w_up)[0]
    w_down = f(w_down)[0]
    conv_w = f(conv_w)[0]
    conv_b = f(conv_b)[0]
    w_spatial = f(w_spatial)[0]
    b_spatial = f(b_spatial)[0]

    perm = np.arange(512).reshape(8, 64)
    perm = np.concatenate([perm[:, 32:], perm[:, :32]], axis=1).reshape(-1)
    Wq = w_in[:, 0:512]
    Wk = w_in[:, 512:1024]
    Wv = w_in[:, 1024:1536]
    Wu = w_in[:, 1536:2048]
    Wvg = w_in[:, 2048:2560]
    shared = {
        "wk": _chunk_rows(np.concatenate([Wk, Wk[:, perm]], axis=1)).reshape(128, -1),
        "wq": _chunk_rows(np.concatenate([Wq, Wq[:, perm]], axis=1)).reshape(128, -1),
        "wv": _chunk_rows(Wv).reshape(128, -1),
        "wvg": _chunk_rows(Wvg).reshape(128, -1),
        "wu": _chunk_rows(Wu).reshape(128, -1),
        "wo": _chunk_rows(w_out).reshape(128, -1),
        "wgu": np.ascontiguousarray(np.stack([w_gate, w_up], axis=0).reshape(2, 8, 128, NFC, 128)
                                    .transpose(3, 2, 0, 1, 4)).reshape(NFC, 128, -1),
        "wdn": np.ascontiguousarray(w_down.reshape(NFC, 128, 1024)),
        "lntab": np.concatenate([_bcast(f(ln1_g)[0]), _bcast(f(ln1_b)[0]), _bcast(f(ln2_g)[0]), _bcast(f(ln2_b)[0])], axis=1),
        "gmtab": np.concatenate([_bcast(f(gmlp_ln_g)[0]), _bcast(f(gmlp_ln_b)[0])], axis=1),
        "bstab": _bcast(b_spatial.reshape(-1)),
        "wsT": np.ascontiguousarray(w_spatial.transpose(2, 0, 1)).reshape(128, -1),
        "convtab": np.ascontiguousarray(np.concatenate([conv_w, conv_b[None, :]], axis=0).reshape(4, NFC, 128)
                                        .transpose(2, 1, 0)).reshape(128, -1),
        "lamtab": np.concatenate([_bcast(f(lambda_q1)[0]), _bcast(f(lambda_k1)[0]),
                                  _bcast(f(lambda_q2)[0]), _bcast(f(lambda_k2)[0])], axis=1),
        "subg": np.ascontiguousarray(f(subln_g)[0].reshape(128, 1)),
    }
    shared = {k: np.ascontiguousarray(v, dtype=np.float32) for k, v in shared.items()}

    per_role = {}
    for r in range(2):
        ktok, flg = _core_tokens(r)
        per_role[r] = dict(
            ktok=ktok,
            rope=_rope_tables(ktok),
            kbias=np.ascontiguousarray(np.broadcast_to(_kbias_table(ktok)[None, :], (128, NKBIAS))).astype(np.float32),
            flags=np.ascontiguousarray(np.broadcast_to(np.asarray(flg, np.float32)[None, :], (128, 2))),
            qtok=np.concatenate([ktok[0:4096], ktok[8192:8448]]),
        )

    in_maps = []
    for c in range(8):
        b, r = c // 2, c % 2
        pr = per_role[r]
        xb = x[b]
        xk = xb[pr["ktok"]]
        xT = np.ascontiguousarray(xk.reshape(NKB, 256, 8, 128).transpose(0, 3, 2, 1)).reshape(NKB, 128, 8 * 256)
        m = dict(shared)
        m["xT"] = xT
        m["xq"] = np.ascontiguousarray(xb[pr["qtok"]])
        m["rope"] = pr["rope"]
        m["kbias"] = pr["kbias"]
        m["flags"] = pr["flags"]
        in_maps.append(m)

    if "nc" not in _NC_CACHE:
        _NC_CACHE["nc"] = build_nc()
    nc = _NC_CACHE["nc"]
    res = run_bass_kernel_spmd(nc, in_maps, core_ids=list(range(8)))
    out = np.empty((NB, S, D), np.float32)
    for c in range(8):
        b, r = c // 2, c % 2
        own = per_role[r]["ktok"][0:4096]
        out[b, own] = res.results[c]["y"]
    return out
```

```python
import contextlib
import numpy as np
import concourse.bass as bass
import concourse.mybir as mybir
from concourse.bass_utils import run_bass_kernel_spmd

F32 = mybir.dt.float32
BF16 = mybir.dt.bfloat16
AF = mybir.ActivationFunctionType
ALU = mybir.AluOpType
AX = mybir.AxisListType

D = 1024
S = 8192
NB = 4
DFF = 2816
NFC = DFF // 128
TB = 256
NKB = 33
NQROW = 4096 + 256
ALPHA = 2.0 ** 0.25
LN_EPS = 1e-5
LAMBDA_INIT = 0.2
NEG = -30000.0
ENG_NAMES = ("tensor", "vector", "scalar", "gpsimd", "sync")


class _Op:
    __slots__ = ("eng", "fn", "deps", "is_dma", "sem_key", "signal", "ticket")

    def __init__(self, eng, fn, is_dma=False, sem_key=None):
        self.eng = eng
        self.fn = fn
        self.deps = []
        self.is_dma = is_dma
        self.sem_key = sem_key
        self.signal = is_dma
        self.ticket = None


class Prog:
    def __init__(self, nc):
        self.nc = nc
        self.ops = []
        self.last_writer = {}
        self.readers = {}

    @staticmethod
    def _flat(keys):
        out = []
        for k in keys:
            if isinstance(k, (list, tuple)):
                out.extend(Prog._flat(k))
            else:
                out.append(k)
        return out

    def _add(self, op, reads, writes):
        reads = self._flat(reads)
        writes = self._flat(writes)
        deps = []
        seen = set()

        def push(d):
            if d is not None and id(d) not in seen:
                seen.add(id(d))
                deps.append(d)

        raw = set()
        for k in reads:
            w = self.last_writer.get(k)
            push(w)
            if w is not None:
                raw.add(id(w))
        for k in writes:
            push(self.last_writer.get(k))
            for r in self.readers.get(k, ()):
                push(r)
        for d in deps:
            same = (d.eng == op.eng) and not d.is_dma and not op.is_dma
            if same:
                if op.eng == "tensor":
                    continue
                if id(d) not in raw:
                    continue
            op.deps.append(d)
            d.signal = True
        for k in reads:
            self.readers.setdefault(k, []).append(op)
        for k in writes:
            self.last_writer[k] = op
            self.readers[k] = []
        self.ops.append(op)
        return op

    def op(self, eng, method, reads, writes, *args, **kwargs):
        fn = lambda e: getattr(e, method)(*args, **kwargs)
        return self._add(_Op(eng, fn), list(reads), list(writes))

    def dma(self, eng, out, in_, reads=(), writes=(), sem_key=None):
        o = _Op(eng, lambda e: e.dma_start(out=out, in_=in_), is_dma=True, sem_key=sem_key)
        return self._add(o, list(reads), list(writes))

    def emit(self, final_wait_ops=()):
        nc = self.nc
        final_wait_ops = list(final_wait_ops)
        for e in ENG_NAMES:
            if e == "sync":
                continue
            lasts = [o for o in self.ops if o.eng == e and not o.is_dma]
            if lasts:
                lasts[-1].signal = True
                final_wait_ops.append(lasts[-1])
        eng_count = {e: 0 for e in ENG_NAMES}
        dma_count = {}
        for o in self.ops:
            if o.is_dma:
                dma_count[o.sem_key] = dma_count.get(o.sem_key, 0) + 16
                o.ticket = dma_count[o.sem_key]
            elif o.signal:
                eng_count[o.eng] += 1
                o.ticket = eng_count[o.eng]
        with contextlib.ExitStack() as st:
            sems = {}
            for e in ENG_NAMES:
                sems["e_" + e] = st.enter_context(nc.semaphore("e_" + e))
            for i, k in enumerate(dma_count):
                sems["d_" + k] = st.enter_context(nc.semaphore("d%d_%s" % (i, k)))
            block = st.enter_context(nc.Block())

            def semof(o):
                return sems["d_" + o.sem_key] if o.is_dma else sems["e_" + o.eng]

            def run_engine(engname):
                def body(eng):
                    waited = {}
                    for o in self.ops:
                        if o.eng != engname:
                            continue
                        for d in o.deps:
                            s = semof(d)
                            if waited.get(id(s), 0) >= d.ticket:
                                continue
                            eng.wait_ge(s, d.ticket)
                            waited[id(s)] = d.ticket
                        ins = o.fn(eng)
                        if o.is_dma:
                            ins.then_inc(semof(o), 16)
                        elif o.signal:
                            ins.then_inc(semof(o), 1)
                    if engname == "sync":
                        for o in final_wait_ops:
                            eng.wait_ge(semof(o), o.ticket)
                return body

            block.tensor(run_engine("tensor"))
            block.vector(run_engine("vector"))
            block.scalar(run_engine("scalar"))
            block.gpsimd(run_engine("gpsimd"))
            block.sync(run_engine("sync"))


def _steps():
    steps = []

    def add(kind, blk, col0, tb, nondiag, diag, qrow, seg, orow):
        steps.append(dict(kind=kind, blk=blk, col0=col0, tb=tb, nondiag=nondiag, diag=diag,
                          qrow=qrow, seg=seg, orow=orow))

    add("H", 32, 0, 128, list(range(32, 48)), [64], 4096, 0, None)
    for s in range(8):
        add("B", s, 0, 256, list(range(0, 2 * s)) + list(range(32, 48)), [2 * s, 2 * s + 1], s * 256, 0, s * 256)
    add("H", 32, 128, 128, list(range(0, 16)) + list(range(32, 64)), [65], 4224, 1, None)
    for s in range(8, 16):
        add("B", s, 0, 256, list(range(0, 2 * s)) + list(range(32, 64)), [2 * s, 2 * s + 1], s * 256, 1, s * 256)
    col = 0
    for st in steps:
        st["kb_col0"] = col
        col += len(st["nondiag"])
    return steps, col


STEPS, NKBIAS = _steps()
_NEED_BIAS = None


def _need_bias():
    global _NEED_BIAS
    if _NEED_BIAS is None:
        nb = np.zeros((NKBIAS,), bool)
        for r in range(2):
            ktok, _ = _core_tokens(r)
            nb |= (_kbias_table(ktok) != 0.0)
        _NEED_BIAS = nb
    return _NEED_BIAS


def _core_tokens(r):
    if r == 0:
        seg0 = np.arange(0, 2048)
        seg1 = np.arange(6144, 8192)
        others = np.arange(2048, 6144)
        h0 = np.arange(0, 128)
        h1 = np.arange(6016, 6144)
        flags = (0.0, 1.0)
    else:
        seg0 = np.arange(2048, 4096)
        seg1 = np.arange(4096, 6144)
        others = np.concatenate([np.arange(0, 2048), np.arange(6144, 8192)])
        h0 = np.arange(1920, 2048)
        h1 = np.arange(3968, 4096)
        flags = (1.0, 1.0)
    ktok = np.concatenate([seg0, seg1, others, h0, h1])
    return ktok, flags


def _kbias_table(ktok):
    kb = np.zeros((NKBIAS,), np.float32)
    for st in STEPS:
        q0 = int(ktok[st["diag"][0] * 128])
        for i, t in enumerate(st["nondiag"]):
            p0 = int(ktok[t * 128])
            valid = (p0 + 128) <= q0
            kb[st["kb_col0"] + i] = 0.0 if valid else NEG
        cov = np.zeros(S, np.int32)
        for i, t in enumerate(st["nondiag"]):
            if kb[st["kb_col0"] + i] == 0.0:
                cov[ktok[t * 128:(t + 1) * 128]] += 1
        if not (st["kind"] == "H" and q0 == 0):
            assert np.all(cov[:q0] == 1) and np.all(cov[q0:] == 0), (st["kind"], st["blk"], q0)
    return kb


def build_nc(debug=False, p1_blocks=NKB, n_steps=None, stop_stage=None):
    nc = bass.Bass("TRN2", target_bir_lowering=False)

    def din(name, shape, dt=F32):
        return nc.dram_tensor(name, list(shape), dt, kind="ExternalInput").ap()

    xT_h = din("xT", [NKB, 128, 8 * 256])
    xq_h = din("xq", [NQROW, D])
    rope_h = din("rope", [NKB, 128, 2 * 256])
    kbias_h = din("kbias", [128, NKBIAS])
    lntab_h = din("lntab", [128, 4 * D])
    gmtab_h = din("gmtab", [128, 2 * 512])
    bstab_h = din("bstab", [128, 4 * 128])
    wsT_h = din("wsT", [128, 4 * 128])
    convtab_h = din("convtab", [128, NFC * 4])
    lamtab_h = din("lamtab", [128, 4 * 64])
    subg_h = din("subg", [128, 1])
    flags_h = din("flags", [128, 2])
    wk_h = din("wk", [128, 8 * 1024])
    wq_h = din("wq", [128, 8 * 1024])
    wv_h = din("wv", [128, 8 * 512])
    wvg_h = din("wvg", [128, 8 * 512])
    wu_h = din("wu", [128, 8 * 512])
    wo_h = din("wo", [128, 8 * 1024])
    wgu_h = din("wgu", [NFC, 128, 2 * 8 * 128])
    wdn_h = din("wdn", [NFC, 128, 1024])
    y_h = nc.dram_tensor("y", [4096, D], F32, kind="ExternalOutput").ap()
    dbg_h = {}
    if debug:
        for nm, shp in (("d_x1", [128, D]), ("d_cat", [128, 8 * 256]), ("d_q", [128, 4 * 256]), ("d_halo", [128, NFC * 2])):
            dbg_h[nm] = nc.dram_tensor(nm, shp, F32, kind="ExternalOutput").ap()

    wgub = nc.dram_tensor("wgub", [NFC, 128, 2 * 8 * 128], BF16).ap()
    wdnb = nc.dram_tensor("wdnb", [NFC, 128, 1024], BF16).ap()
    kvh = nc.dram_tensor("kvh", [4, NKB, 128, 512], BF16).ap()

    P = Prog(nc)
    need_bias = _need_bias()

    def sb(name, shape, dt):
        return nc.alloc_sbuf_tensor("s_" + name, list(shape), dt)

    WA = sb("WA", [128, 8, 1024], BF16)
    WV = sb("WV", [128, 8, 512], BF16)
    WU = sb("WU", [128, 8, 512], BF16)
    WO = sb("WO", [128, 8, 1024], BF16)
    lntab = sb("lntab", [128, 4, D], F32)
    gmtab = sb("gmtab", [128, 2, 512], F32)
    bstab = sb("bstab", [128, 4, 128], F32)
    wsT = sb("wsT", [128, 4, 128], BF16)
    convtab = sb("convtab", [128, NFC, 4], F32)
    lamtab = sb("lamtab", [128, 4, 64], F32)
    subg = sb("subg", [128, 1], F32)
    flags = sb("flags", [128, 2], F32)
    kbias = sb("kbias", [128, NKBIAS], F32)
    ident = sb("ident", [128, 128], F32)
    ones_b = sb("ones_b", [128, 128], BF16)
    mkL = sb("mkL", [64, 128], BF16)
    mkRH = sb("mkRH", [64, 256], BF16)
    mkRB = sb("mkRB", [64, 2, 512], BF16)
    smalls = sb("smalls", [128, 16], F32)
    lamprod = sb("lamprod", [128, 2, 64], F32)
    halo = sb("halo", [128, NFC, 2], F32)

    xTbuf = [sb("xTbuf%d" % i, [128, 8, 256], BF16) for i in range(2)]
    xstage = [sb("xstage%d" % i, [128, 1024], F32) for i in range(2)]
    ropebuf = [sb("ropebuf%d" % i, [128, 2, 256], F32) for i in range(2)]
    rtmp = [sb("rtmp%d" % i, [128, 2, 256], F32) for i in range(2)]
    qbdB = sb("qbdB", [128, 4, 512], BF16)
    qbdH = sb("qbdH", [128, 4, 256], BF16)
    NKVB = 3
    kvbuf = [sb("kvbuf%d" % i, [128, 4, 512], BF16) for i in range(NKVB)]
    kvout = kvbuf[0:2]
    NEB = 6
    Ebuf = [sb("Ebuf%d" % i, [128, 512], BF16) for i in range(NEB)]
    NZS = 3
    zsum = [sb("zsum%d" % i, [128, 512], BF16) for i in range(NZS)]
    rz = sb("rz", [128, 512], F32)
    onb = sb("onb", [128, 512], F32)
    oa = sb("oa", [128, 256], F32)
    sqb = sb("sqb", [128, 256], BF16)
    lnt = sb("lnt", [128, 256], F32)
    rstd = sb("rstd", [128, 256], F32)
    catT = sb("catT", [128, 8, 256], BF16)
    uT = sb("uT", [128, 4, 256], F32)
    vg = [sb("vg%d" % i, [128, 512], F32) for i in range(2)]
    vgn = sb("vgn", [128, 512], F32)
    vgnb = [sb("vgnb%d" % i, [128, 512], BF16) for i in range(2)]
    gst = sb("gst", [128, 4, 6], F32)
    gmv = sb("gmv", [128, 4, 2], F32)
    glnv = sb("glnv", [128, 4], F32)
    grs = sb("grs", [128, 4], F32)
    gtmp = [sb("gtmp%d" % i, [128, 128], F32) for i in range(2)]
    xres = [sb("xres%d" % i, [128, D], F32) for i in range(2)]
    x1 = [sb("x1_0", [128, 2, D], F32)] * 2
    x1T = sb("x1T", [128, 8, 256], BF16)
    lst = sb("lst", [128, 2, 6], F32)
    lmv = sb("lmv", [128, 2], F32)
    llnv = sb("llnv", [128, 1], F32)
    lrs = sb("lrs", [128, 1], F32)
    gbuf = [sb("gbuf%d" % i, [128, 258], F32) for i in range(2)]
    cbuf = [sb("cbuf%d" % i, [128, 256], F32) for i in range(2)]
    sbuf_ = [sb("sbuf%d" % i, [128, 256], F32) for i in range(2)]
    NHT = 4
    hT = [sb("hT%d" % i, [128, 256], BF16) for i in range(NHT)]
    NWG = 4
    wgubuf = [sb("wgubuf%d" % i, [128, 2, 8, 128], BF16) for i in range(NWG)]
    wdnbuf = [sb("wdnbuf%d" % i, [128, 1024], BF16) for i in range(NWG)]
    dbgbuf = sb("dbgbuf", [128, 8 * 256], F32) if debug else None

    pp = [nc.alloc_psum_tensor("pp%d" % i, [128, 1024], F32) for i in range(4)]
    ps = [pp[i // 2][:, (i % 2) * 512:(i % 2 + 1) * 512] for i in range(8)]

    def psk(i):
        return "ps%d" % i

    class Rot:
        def __init__(self, items):
            self.items = list(items)
            self.i = 0

        def next(self):
            v = self.items[self.i % len(self.items)]
            self.i += 1
            return v

    rotS = Rot([0, 1, 2, 3])
    rotV = Rot([4, 5, 6, 7])

    def mm(out, lhsT, rhs, start, stop, reads, writes, **kw):
        P.op("tensor", "matmul", reads, writes, out, lhsT=lhsT, rhs=rhs, start=start, stop=stop, **kw)

    def act(out, in_, func, reads, writes, **kw):
        P.op("scalar", "activation", reads, writes, out=out, in_=in_, func=func, **kw)

    def tt(eng, out, in0, in1, op, reads, writes):
        P.op(eng, "tensor_tensor", reads, writes, out=out, in0=in0, in1=in1, op=op)

    def ts2(eng, out, in0, s1, s2, op0, op1, reads, writes):
        P.op(eng, "tensor_scalar", reads, writes, out=out, in0=in0, scalar1=s1, scalar2=s2, op0=op0, op1=op1)

    def stt(eng, out, in0, scalar, in1, op0, op1, reads, writes):
        P.op(eng, "scalar_tensor_tensor", reads, writes, out=out, in0=in0, scalar=scalar, in1=in1, op0=op0, op1=op1)

    def memset(ap, val, reads, writes):
        P.op("gpsimd", "memset", reads, writes, ap, val)

    cnt = dict(e=0, kv=0, wg=0, ht=0, gb=0, xr=0, vg=0, gt=0, oz=0, s=0, zs=0, xs=0)

    def nxt(k, mod):
        v = cnt[k] % mod
        cnt[k] += 1
        return v

    eps_c = smalls[:, 0:1]
    memset(smalls[:], 0.0, [], ["smalls"])
    memset(smalls[:, 0:1], LN_EPS, ["smalls"], ["smalls"])
    memset(ident[:], 0.0, [], ["ident"])
    P.op("gpsimd", "affine_select", ["ident"], ["ident"], out=ident[:], in_=ident[:], pattern=[[-1, 128]],
         compare_op=ALU.not_equal, fill=1.0, base=0, channel_multiplier=1)
    memset(ones_b[:], 1.0, [], ["ones_b"])
    memset(mkL[:], 0.0, [], ["mk"])
    memset(mkL[0:1, :], 1.0, ["mk"], ["mk"])
    memset(mkL[32:33, 64:128], 1.0, ["mk"], ["mk"])
    memset(mkRH[:], 0.0, ["mk"], ["mk"])
    memset(mkRB[:], 0.0, ["mk"], ["mk"])
    for a in range(2):
        memset(mkRH[32:33, a * 128:a * 128 + 64], NEG, ["mk"], ["mk"])
        memset(mkRB[32:33, 0, a * 256:a * 256 + 64], NEG, ["mk"], ["mk"])
        memset(mkRB[0:1, 1, a * 256:a * 256 + 128], NEG, ["mk"], ["mk"])
        memset(mkRB[32:33, 1, a * 256 + 128:a * 256 + 192], NEG, ["mk"], ["mk"])
    memset(qbdB[:], 0.0, [], ["qbd"])
    memset(qbdH[:], 0.0, [], ["qbd"])
    memset(halo[:], 0.0, [], ["halo"])

    def convert_ffn_piece(pc):
        a_, b_ = 2 * pc, 2 * pc + 2
        P.dma("gpsimd", wgub[a_:b_], wgu_h[a_:b_], writes=["wgub%d" % pc], sem_key="wgub%d" % pc)
        P.dma("gpsimd", wdnb[a_:b_], wdn_h[a_:b_], writes=["wdnb%d" % pc], sem_key="wdnb%d" % pc)

    def load_cast(dst, dst_key, src, ncols, engines):
        for ci, c in enumerate(range(0, ncols, 1024)):
            xs_ = nxt("xs", 2)
            sk_ = "xstage%d" % xs_
            P.dma("sync", xstage[xs_][:], src[:, c:c + 1024], writes=[sk_], sem_key=sk_)
            eng = engines[ci % len(engines)]
            if eng == "scalar":
                act(dst[:, c:c + 1024], xstage[xs_][:], AF.Copy, [sk_, dst_key], [dst_key])
            else:
                P.op(eng, "tensor_copy", [sk_, dst_key], [dst_key], out=dst[:, c:c + 1024], in_=xstage[xs_][:])

    load_cast(WA[:].rearrange("p k c -> p (k c)"), "WA", wk_h, 8192, ("scalar", "vector"))
    load_cast(WV[:].rearrange("p k c -> p (k c)"), "WV", wv_h, 4096, ("scalar", "vector"))

    def ld(dst, src, key):
        P.dma("sync", dst, src, writes=[key], sem_key=key)

    ld(kbias[:], kbias_h, "kbias")
    ld(lamtab[:].rearrange("p a b -> p (a b)"), lamtab_h, "lamtab")
    ld(subg[:], subg_h, "subg")
    ld(flags[:], flags_h, "flags")
    ld(convtab[:].rearrange("p a b -> p (a b)"), convtab_h, "convtab")
    wsT32 = vgn[:].rearrange("p (a b) -> p a b", a=4)
    ld(vgn[:], wsT_h, "vgn")
    ld(bstab[:].rearrange("p a b -> p (a b)"), bstab_h, "bstab")
    ld(gmtab[:].rearrange("p a b -> p (a b)"), gmtab_h, "gmtab")
    ld(lntab[:].rearrange("p a b -> p (a b)"), lntab_h, "lntab")

    P.op("gpsimd", "affine_select", ["vgn"], ["vgn"], out=wsT32, in_=wsT32, pattern=[[0, 4], [1, 128]],
         compare_op=ALU.is_ge, fill=0.0, base=0, channel_multiplier=-1)
    P.op("vector", "tensor_copy", ["vgn"], ["wsT"], out=wsT[:], in_=wsT32)

    tt("vector", lamprod[:, 0, :], lamtab[:, 0, :], lamtab[:, 1, :], ALU.mult, ["lamtab"], ["lamprod"])
    tt("vector", lamprod[:, 1, :], lamtab[:, 2, :], lamtab[:, 3, :], ALU.mult, ["lamtab", "lamprod"], ["lamprod"])
    P.op("vector", "reduce_sum", ["lamprod", "smalls"], ["smalls"], out=smalls[:, 1:2], in_=lamprod[:, 0, :], axis=AX.X)
    P.op("vector", "reduce_sum", ["lamprod", "smalls"], ["smalls"], out=smalls[:, 2:3], in_=lamprod[:, 1, :], axis=AX.X)
    act(smalls[:, 3:5], smalls[:, 1:3], AF.Exp, ["smalls"], ["smalls"])
    tt("vector", smalls[:, 5:6], smalls[:, 4:5], smalls[:, 3:4], ALU.subtract, ["smalls"], ["smalls"])
    ts2("vector", smalls[:, 6:7], smalls[:, 5:6], -LAMBDA_INIT, 1.0, ALU.add, ALU.mult, ["smalls"], ["smalls"])
    ts2("vector", smalls[:, 7:8], subg[:], 1.0 - LAMBDA_INIT, 0.0, ALU.mult, ALU.add, ["smalls", "subg"], ["smalls"])
    neg_lam = smalls[:, 6:7]
    g08 = smalls[:, 7:8]

    def load_xblock(kb, buf, engines=("scalar",)):
        load_cast(xTbuf[buf][:].rearrange("p k t -> p (k t)"), "xTbuf%d" % buf, xT_h[kb], 2048, engines)
        P.dma("sync", ropebuf[buf][:].rearrange("p a t -> p (a t)"), rope_h[kb],
              writes=["ropebuf%d" % buf], sem_key="ropebuf%d" % buf)

    rt_i = [0]

    def proj_rope(xt, xkey, rp, rpkey, c0, n, cc, outs, out_key):
        bank = rotS.next()
        for half in range(2):
            for kc in range(8):
                mm(ps[bank][:, half * 256:half * 256 + n],
                   WA[:, kc, half * 512 + cc * 128:half * 512 + (cc + 1) * 128],
                   xt[:, kc, c0:c0 + n], kc == 0, kc == 7, ["WA", xkey], [psk(bank)])
        ti = rt_i[0] % 2
        rt_i[0] += 1
        tk = "rtmp%d" % ti
        tt("vector", rtmp[ti][:, 0, 0:n], ps[bank][:, 0:n], rp[:, 0, c0:c0 + n], ALU.mult, [psk(bank), rpkey], [tk])
        tt("vector", rtmp[ti][:, 1, 0:n], ps[bank][:, 256:256 + n], rp[:, 1, c0:c0 + n], ALU.mult,
           [psk(bank), rpkey, tk], [tk])
        for (r0, r1, out_ap) in outs:
            tt("gpsimd", out_ap, rtmp[ti][r0:r1, 0, 0:n], rtmp[ti][r0:r1, 1, 0:n], ALU.add, [tk, out_key], [out_key])

    load_xblock(0, 0)
    for kb in range(p1_blocks):
        buf = kb % 2
        if kb + 1 < p1_blocks:
            load_xblock(kb + 1, (kb + 1) % 2)
        if 1 <= kb <= NFC // 2:
            convert_ffn_piece(kb - 1)
        if kb == 2:
            load_cast(WU[:].rearrange("p k c -> p (k c)"), "WU", wu_h, 4096, ("scalar",))
        if kb == 4:
            load_cast(WO[:].rearrange("p k c -> p (k c)"), "WO", wo_h, 8192, ("scalar",))
        xt = xTbuf[buf]
        rp = ropebuf[buf]
        ko = kvout[buf]
        kok = "kvbuf%d" % buf
        for cc in range(4):
            proj_rope(xt, "xTbuf%d" % buf, rp, "ropebuf%d" % buf, 0, 256, cc, [(0, 128, ko[:, cc, 0:256])], kok)
        for t in range(2):
            bank = rotV.next()
            for kc in range(8):
                mm(ps[bank][:, 0:512], xt[:, kc, t * 128:(t + 1) * 128], WV[:, kc, :], kc == 0, kc == 7,
                   ["WV", "xTbuf%d" % buf], [psk(bank)])
            act(ko[:, :, 256 + t * 128:256 + (t + 1) * 128], ps[bank].rearrange("p (h d) -> p h d", h=4), AF.Copy,
                [psk(bank), kok], [kok])
        P.dma("sync", kvh[:, kb].rearrange("h p c -> p h c"), ko[:], reads=[kok], writes=["kvh%d" % kb], sem_key=kok)

    load_cast(WA[:].rearrange("p k c -> p (k c)"), "WA", wq_h, 8192, ("scalar", "vector"))
    load_cast(WV[:].rearrange("p k c -> p (k c)"), "WV", wvg_h, 4096, ("scalar", "vector"))

    final_ops = []

    def group_blocks(blocks):
        groups = []
        cur = []
        for b in blocks:
            if cur and (b != cur[-1] + 1 or len(cur) == 4):
                groups.append(cur)
                cur = []
            cur.append(b)
        if cur:
            groups.append(cur)
        return groups

    def emit_ln(xb_, xk_, t, gi, bi_):
        for hf in range(2):
            P.op("vector", "bn_stats", [xk_, "lst"], ["lst"], out=lst[:, hf, :], in_=xb_[:, t, hf * 512:(hf + 1) * 512])
        P.op("vector", "bn_aggr", ["lst"], ["lmv"], out=lmv[:], in_=lst[:].rearrange("p a b -> p (a b)"))
        act(llnv[:], lmv[:, 1:2], AF.Ln, ["lmv", "smalls"], ["llnv"], bias=eps_c, scale=1.0)
        act(lrs[:], llnv[:], AF.Exp, ["llnv"], ["lrs"], scale=-0.5)
        stt("vector", xb_[:, t, :], xb_[:, t, :], lmv[:, 0:1], lntab[:, gi, :], ALU.subtract, ALU.mult,
            [xk_, "lmv", "lntab"], [xk_])
        stt("vector", xb_[:, t, :], xb_[:, t, :], lrs[:, 0:1], lntab[:, bi_, :], ALU.mult, ALU.add,
            [xk_, "lrs", "lntab"], [xk_])

    def make_kv_stream(st):
        nd_blocks = sorted(set(t // 2 for t in st["nondiag"]))
        vis = {}
        for i, t in enumerate(st["nondiag"]):
            vis.setdefault(t // 2, []).append((t, False, 0, st["kb_col0"] + i))
        dblk = st["diag"][0] // 2
        dvis = [(t, True, di, None) for di, t in enumerate(st["diag"])]
        GL = []
        per_head = []
        for h in range(4):
            tiles = []
            for g in group_blocks(nd_blocks):
                kvb = nxt("kv", NKVB)
                GL.append((h, g, kvb))
                for bi, b in enumerate(g):
                    for v in vis[b]:
                        tiles.append((v, kvb, bi, len(GL) - 1))
            kvb = nxt("kv", NKVB)
            GL.append((h, [dblk], kvb))
            for v in dvis:
                tiles.append((v, kvb, 0, len(GL) - 1))
            per_head.append(tiles)
        glast = {}
        for h in range(4):
            for ti_, tl in enumerate(per_head[h]):
                glast[tl[3]] = (h, ti_)
        issued = [0]

        def issue(upto):
            while issued[0] < min(upto, len(GL)):
                h_, g, kvb_ = GL[issued[0]]
                P.dma("sync", kvbuf[kvb_][:, 0:len(g), :], kvh[h_, g[0]:g[0] + len(g)].rearrange("b p c -> p b c"),
                      reads=["kvh%d" % b for b in g], writes=["kvbuf%d" % kvb_], sem_key="kvbuf%d" % kvb_)
                issued[0] += 1

        def done_upto(h, local_idx):
            gd = -1
            for gi_ in range(len(GL)):
                hh, tt_ = glast[gi_]
                if hh < h or (hh == h and tt_ <= local_idx):
                    gd = gi_
                else:
                    break
            return gd

        return dict(per_head=per_head, issue=issue, done_upto=done_upto)

    def attention_head(st, h, tb, isH, kvs, deferred):
        ozs = nxt("oz", 2)
        bO = 4 + 2 * ozs
        bZ = 5 + 2 * ozs
        n2 = 2 * tb
        qbd = qbdH if isH else qbdB
        tiles = kvs["per_head"][h]
        ntile = len(tiles)
        info = []

        def emit_S(idx):
            (t, isd, di, bcol), kvb_, bi, gidx = tiles[idx]
            half = t % 2
            bank = rotS.next()
            ebi = nxt("e", NEB)
            mm(ps[bank][:, 0:n2], kvbuf[kvb_][:, bi, half * 128:(half + 1) * 128], qbd[:, h, 0:n2], True, not isd,
               ["kvbuf%d" % kvb_, "qbd"], [psk(bank)])
            if isd:
                mrhs = mkRH[0:33, 0:n2] if isH else mkRB[0:33, di, 0:n2]
                mm(ps[bank][:, 0:n2], mkL[0:33, :], mrhs, False, True, ["mk"], [psk(bank)])
                act(Ebuf[ebi][:, 0:n2], ps[bank][:, 0:n2], AF.Exp, [psk(bank)], ["Ebuf%d" % ebi], scale=0.125)
            elif need_bias[bcol]:
                act(Ebuf[ebi][:, 0:n2], ps[bank][:, 0:n2], AF.Exp, [psk(bank), "kbias"], ["Ebuf%d" % ebi],
                    bias=kbias[:, bcol:bcol + 1], scale=0.125)
            else:
                act(Ebuf[ebi][:, 0:n2], ps[bank][:, 0:n2], AF.Exp, [psk(bank)], ["Ebuf%d" % ebi], scale=0.125)
            info.append(ebi)

        ZG = 8
        zstate = dict(first=True)

        def emit_PV(idx):
            (t, isd, di, bcol), kvb_, bi, gidx = tiles[idx]
            half = t % 2
            ebi = info[idx]
            first = idx == 0
            last = idx == ntile - 1
            mm(ps[bO][:, 0:n2], kvbuf[kvb_][:, bi, 256 + half * 128:256 + (half + 1) * 128], Ebuf[ebi][:, 0:n2],
               first, last, ["kvbuf%d" % kvb_, "Ebuf%d" % ebi], [psk(bO)])
            if idx % ZG == 0:
                zstate["cur"], zstate["ckey"] = Ebuf[ebi], "Ebuf%d" % ebi
            else:
                zi = nxt("zs", NZS)
                tt("vector", zsum[zi][:, 0:n2], zstate["cur"][:, 0:n2], Ebuf[ebi][:, 0:n2], ALU.add,
                   [zstate["ckey"], "Ebuf%d" % ebi], ["zsum%d" % zi])
                zstate["cur"], zstate["ckey"] = zsum[zi], "zsum%d" % zi
            if (idx % ZG == ZG - 1) or last:
                mm(ps[bZ][:, 0:n2], ones_b[:], zstate["cur"][:, 0:n2], zstate["first"], last,
                   ["ones_b", zstate["ckey"]], [psk(bZ)])
                zstate["first"] = False

        LOOK = 3
        if h == 0:
            kvs["issue"](NKVB)
        for idx in range(ntile + LOOK):
            if idx < ntile:
                emit_S(idx)
            if idx - LOOK >= 0:
                emit_PV(idx - LOOK)
                kvs["issue"](kvs["done_upto"](h, idx - LOOK) + 1 + NKVB)
            if idx == 6 and deferred is not None:
                deferred()
                deferred = None
        if deferred is not None:
            deferred()

        def epilogue():
          P.op("vector", "reciprocal", [psk(bZ)], ["rz"], out=rz[:, 0:n2], in_=ps[bZ][:, 0:n2])
          tt("vector", onb[:, 0:n2], ps[bO][:, 0:n2], rz[:, 0:n2], ALU.mult, [psk(bO), "rz"], ["onb"])
          stt("vector", oa[:, 0:tb], onb[:, tb:n2], neg_lam, onb[:, 0:tb], ALU.mult, ALU.add, ["onb", "smalls"], ["oa"])
          tt("gpsimd", sqb[:, 0:tb], oa[:, 0:tb], oa[:, 0:tb], ALU.mult, ["oa"], ["sqb"])
          mm(ps[bZ][:, 0:tb], ones_b[:], sqb[:, 0:tb], True, True, ["ones_b", "sqb"], [psk(bZ)])
          act(lnt[:, 0:tb], ps[bZ][:, 0:tb], AF.Ln, [psk(bZ), "smalls"], ["lnt"], bias=eps_c, scale=1.0 / 128.0)
          act(rstd[:, 0:tb], lnt[:, 0:tb], AF.Exp, ["lnt"], ["rstd"], scale=-0.5)
          stt("vector", catT[:, h, 0:tb], oa[:, 0:tb], g08, rstd[:, 0:tb], ALU.mult, ALU.mult,
              ["oa", "rstd", "smalls", "catT%d" % h], ["catT%d" % h])

        return epilogue

    def emit_front(si):
        st = step_list[si]
        tb = st["tb"]
        nt = tb // 128
        isH = st["kind"] == "H"
        c0q = st["col0"]
        xb_i = si % 2
        x1b = x1[si % 2]
        x1k = "x1_0"
        xkey = "xTbuf%d" % xb_i
        rpkey = "ropebuf%d" % xb_i
        xt = xTbuf[xb_i]
        rp = ropebuf[xb_i]
        if si == 0:
            load_xblock(st["blk"], xb_i)
        qbd = qbdH if isH else qbdB
        xis = []
        for t in range(nt):
            xi = nxt("xr", 2)
            xis.append(xi)
            r0 = st["qrow"] + t * 128
            P.dma("sync", xres[xi][:], xq_h[r0:r0 + 128, :], writes=["xres%d" % xi], sem_key="xres%d" % xi)

        for cc in range(4):
            proj_rope(xt, xkey, rp, rpkey, c0q, tb, cc,
                      [(0, 64, qbd[0:64, cc, 0:tb]), (64, 128, qbd[64:128, cc, tb:2 * tb])], "qbd")

        for g in range(4):
            bank = rotV.next()
            for kc in range(8):
                mm(ps[bank][:, 0:tb], WU[:, kc, g * 128:(g + 1) * 128], xt[:, kc, c0q:c0q + tb], kc == 0, kc == 7,
                   ["WU", xkey], [psk(bank)])
            act(uT[:, g, 0:tb], ps[bank][:, 0:tb], AF.Gelu, [psk(bank), "uT"], ["uT"])
        vis_ = []
        for t in range(nt):
            bank = rotV.next()
            vi = nxt("vg", 2)
            vis_.append(vi)
            vk = "vg%d" % vi
            for kc in range(8):
                mm(ps[bank][:, 0:512], xt[:, kc, c0q + t * 128:c0q + (t + 1) * 128], WV[:, kc, :], kc == 0, kc == 7,
                   ["WV", xkey], [psk(bank)])
            act(vg[vi][:], ps[bank], AF.Gelu, [psk(bank)], [vk])
        if si + 1 < len(step_list):
            load_xblock(step_list[si + 1]["blk"], (si + 1) % 2, engines=("vector",))
        def gmlp_ln():
          for t in range(nt):
            vi = vis_[t]
            vk = "vg%d" % vi
            for g in range(4):
                P.op("vector", "bn_stats", [vk, "gst"], ["gst"], out=gst[:, g, :], in_=vg[vi][:, g * 128:(g + 1) * 128])
            for g in range(4):
                P.op("vector", "bn_aggr", ["gst", "gmv"], ["gmv"], out=gmv[:, g, :], in_=gst[:, g, :])
            act(glnv[:], gmv[:, :, 1], AF.Ln, ["gmv", "smalls"], ["glnv"], bias=eps_c, scale=1.0)
            act(grs[:], glnv[:], AF.Exp, ["glnv"], ["grs"], scale=-0.5)
            for g in range(4):
                gs = slice(g * 128, (g + 1) * 128)
                stt("vector", vgn[:, gs], vg[vi][:, gs], gmv[:, g, 0:1], gmtab[:, 0, gs], ALU.subtract, ALU.mult,
                    [vk, "gmv", "gmtab", "vgn"], ["vgn"])
                stt("vector", vgnb[vi][:, gs], vgn[:, gs], grs[:, g:g + 1], gmtab[:, 1, gs], ALU.mult, ALU.add,
                    ["vgn", "grs", "gmtab", "vgnb%d" % vi], ["vgnb%d" % vi])

        def gmlp_part2():
            for t in range(nt):
                vi = vis_[t]
                gbank = rotS.next()
                for g in range(4):
                    mm(ps[gbank][:, g * 128:(g + 1) * 128], vgnb[vi][:, g * 128:(g + 1) * 128], wsT[:, g, :], True, True,
                       ["vgnb%d" % vi, "wsT"], [psk(gbank)])
                for g in range(4):
                    gi = nxt("gt", 2)
                    tt("vector", gtmp[gi][:, 0:128], ps[gbank][:, g * 128:(g + 1) * 128], bstab[:, g, :], ALU.add,
                       [psk(gbank), "bstab"], ["gtmp%d" % gi])
                    tt("gpsimd", catT[:, 4 + g, t * 128:(t + 1) * 128], gtmp[gi][:, 0:128], uT[:, g, t * 128:(t + 1) * 128],
                       ALU.mult, ["gtmp%d" % gi, "uT", "catT%d" % (4 + g)], ["catT%d" % (4 + g)])

        return dict(gmlp_part2=gmlp_part2, xis=xis, gmlp_ln=gmlp_ln)

    step_list = STEPS if n_steps is None else STEPS[:n_steps]
    fctx = {}
    if step_list:
        fctx[0] = emit_front(0)
        fctx[0]["gmlp_ln"]()
    for si, st in enumerate(step_list):
        tb = st["tb"]
        nt = tb // 128
        isH = st["kind"] == "H"
        x1b = x1[0]
        x1k = "x1_0"
        gmlp_part2 = fctx[si]["gmlp_part2"]
        xis = fctx[si]["xis"]
        kvs = make_kv_stream(st)
        pend = None
        for h in range(4):
            if h == 1:
                ep0 = pend

                def pend(ep0=ep0):
                    ep0()
                    gmlp_part2()
            pend = attention_head(st, h, tb, isH, kvs, pend)
        pend()

        bank_of = []
        for t in range(nt):
            banks = (rotV.next(), rotV.next())
            bank_of.append(banks)
            ec_order = (4, 5, 6, 7, 0, 1, 2, 3)
            for hf in range(2):
                for eo, ec in enumerate(ec_order):
                    mm(ps[banks[hf]][:, 0:512], catT[:, ec, t * 128:(t + 1) * 128], WO[:, ec, hf * 512:(hf + 1) * 512],
                       eo == 0, eo == 7, ["catT%d" % ec, "WO"], [psk(banks[hf])])
        for t in range(nt):
            xi = xis[t]
            banks = bank_of[t]
            for hf in range(2):
                stt("vector", x1b[:, t, hf * 512:(hf + 1) * 512], xres[xi][:, hf * 512:(hf + 1) * 512], ALPHA,
                    ps[banks[hf]][:, 0:512], ALU.mult, ALU.add, ["xres%d" % xi, psk(banks[hf]), x1k], [x1k])
            emit_ln(x1b, x1k, t, 0, 1)
        if si + 1 < len(step_list):
            fctx[si + 1] = emit_front(si + 1)
        for t in range(nt):
            for half4 in range(2):
                bank = rotS.next()
                for j in range(4):
                    kc = half4 * 4 + j
                    P.op("tensor", "transpose", [x1k, "ident"], [psk(bank)], out=ps[bank][:, j * 128:(j + 1) * 128],
                         in_=x1b[:, t, kc * 128:(kc + 1) * 128], identity=ident[:])
                act(x1T[:, half4 * 4:(half4 + 1) * 4, t * 128:(t + 1) * 128],
                    ps[bank].rearrange("p (j c) -> p j c", j=4), AF.Copy, [psk(bank), "x1T"], ["x1T"])
        if si + 1 < len(step_list):
            fctx[si + 1]["gmlp_ln"]()

        if debug and si == 1:
            o = P.dma("sync", dbg_h["d_x1"], x1b[:, 0, :], reads=[x1k], writes=["d_x1"], sem_key="d_x1")
            final_ops.append(o)
            P.op("vector", "tensor_copy", ["catT"], ["dbgbuf"], out=dbgbuf[:], in_=catT[:].rearrange("p a b -> p (a b)"))
            o = P.dma("sync", dbg_h["d_cat"], dbgbuf[:], reads=["dbgbuf"], writes=["d_cat"], sem_key="d_cat")
            final_ops.append(o)
            o = P.dma("sync", dbg_h["d_halo"], halo[:].rearrange("p a b -> p (a b)"), reads=["halo"], writes=["d_halo"], sem_key="d_halo")
            final_ops.append(o)

        if isH:
            hb = rotV.next()
            for fc in range(NFC):
                wi = nxt("wg", NWG)
                P.dma("sync", wgubuf[wi][:, 0, :, :].rearrange("p k f -> p (k f)"), wgub[fc][:, 0:1024], reads=["wgub%d" % (fc // 2)],
                      writes=["wgubuf%d" % wi], sem_key="wgubuf%d" % wi)
                for kc in range(8):
                    mm(ps[hb][:, 2 * fc:2 * fc + 2], wgubuf[wi][:, 0, kc, :], x1T[:, kc, 126:128], kc == 0, kc == 7,
                       ["wgubuf%d" % wi, "x1T"], [psk(hb)])
            seg = st["seg"]
            ts2("vector", halo[:].rearrange("p a b -> p (a b)"), ps[hb][:, 0:2 * NFC], flags[:, seg:seg + 1], 0.0,
                ALU.mult, ALU.add, [psk(hb), "flags", "halo"], ["halo"])
            continue

        accb = (4, 5, 6, 7)
        pending = []

        def emit_down(fc, hi, wi):
            for t in range(2):
                for hf in range(2):
                    b = accb[t * 2 + hf]
                    mm(ps[b][:, 0:512], hT[hi][:, t * 128:(t + 1) * 128], wdnbuf[wi][:, hf * 512:(hf + 1) * 512],
                       fc == 0, fc == NFC - 1, ["hT%d" % hi, "wdnbuf%d" % wi], [psk(b)], skip_group_check=True)

        for fc in range(NFC):
            wi = nxt("wg", NWG)
            P.dma("sync", wgubuf[wi][:].rearrange("p a k f -> p (a k f)"), wgub[fc], reads=["wgub%d" % (fc // 2)],
                  writes=["wgubuf%d" % wi], sem_key="wgubuf%d" % wi)
            P.dma("sync", wdnbuf[wi][:], wdnb[fc], reads=["wdnb%d" % (fc // 2)], writes=["wdnbuf%d" % wi], sem_key="wdnbuf%d" % wi)
            bank = rotS.next()
            for a in range(2):
                for kc in range(8):
                    mm(ps[bank][:, a * 256:(a + 1) * 256], wgubuf[wi][:, a, kc, :], x1T[:, kc, :], kc == 0, kc == 7,
                       ["wgubuf%d" % wi, "x1T"], [psk(bank)])
            if len(pending) >= 2:
                emit_down(*pending.pop(0))
            gi = nxt("gb", 2)
            gk = "gbuf%d" % gi
            ck = "cbuf%d" % gi
            sk = "sbuf%d" % gi
            P.op("gpsimd", "tensor_copy", ["halo", gk], [gk], out=gbuf[gi][:, 0:2], in_=halo[:, fc, :])
            act(gbuf[gi][:, 2:258], ps[bank][:, 0:256], AF.Copy, [psk(bank), gk], [gk])
            P.op("gpsimd", "tensor_copy", [gk, "halo"], ["halo"], out=halo[:, fc, :], in_=gbuf[gi][:, 256:258])
            ts2("vector", cbuf[gi][:], gbuf[gi][:, 2:258], convtab[:, fc, 2:3], convtab[:, fc, 3:4], ALU.mult, ALU.add,
                [gk, "convtab"], [ck])
            stt("vector", cbuf[gi][:], gbuf[gi][:, 1:257], convtab[:, fc, 1:2], cbuf[gi][:], ALU.mult, ALU.add,
                [gk, ck, "convtab"], [ck])
            stt("vector", cbuf[gi][:], gbuf[gi][:, 0:256], convtab[:, fc, 0:1], cbuf[gi][:], ALU.mult, ALU.add,
                [gk, ck, "convtab"], [ck])
            act(sbuf_[gi][:], cbuf[gi][:], AF.Silu, [ck], [sk])
            hi = nxt("ht", NHT)
            tt("vector", hT[hi][:], sbuf_[gi][:], ps[bank][:, 256:512], ALU.mult, [sk, psk(bank)], ["hT%d" % hi])
            pending.append((fc, hi, wi))
        while pending:
            emit_down(*pending.pop(0))

        for t in range(2):
            for hf in range(2):
                b = accb[t * 2 + hf]
                stt("vector", x1b[:, t, hf * 512:(hf + 1) * 512], x1b[:, t, hf * 512:(hf + 1) * 512], ALPHA,
                    ps[b][:, 0:512], ALU.mult, ALU.add, [x1k, psk(b)], [x1k])
            emit_ln(x1b, x1k, t, 2, 3)
        o = P.dma("sync", y_h[st["orow"]:st["orow"] + 256, :].rearrange("(t p) d -> p t d", p=128), x1b[:],
                  reads=[x1k], writes=["y%d" % si], sem_key=x1k)
        final_ops.append(o)

    P.emit(final_wait_ops=final_ops)
    return nc


def _rope_tables(ktok):
    pos = ktok.astype(np.float32)
    inv_freq = (1.0 / (np.float32(10000.0) ** (np.arange(0, 64, 2, dtype=np.float32) / np.float32(64.0)))).astype(np.float32)
    ang = (pos[:, None] * inv_freq[None, :]).astype(np.float32)
    cos = np.cos(ang).astype(np.float32)
    sin = np.sin(ang).astype(np.float32)
    j = np.arange(128) % 64
    f = j % 32
    sign = np.where(j < 32, -1.0, 1.0).astype(np.float32)
    cosT = cos[:, f].T
    sinT = (sin[:, f] * sign[None, :]).T
    T = ktok.shape[0]
    tab = np.stack([cosT.reshape(128, T // 256, 256), sinT.reshape(128, T // 256, 256)], axis=2)
    return np.ascontiguousarray(tab.transpose(1, 0, 2, 3)).reshape(T // 256, 128, 512).astype(np.float32)


def _chunk_rows(w):
    K, N = w.shape
    return np.ascontiguousarray(w.reshape(K // 128, 128, N).transpose(1, 0, 2))


def _bcast(v):
    return np.ascontiguousarray(np.broadcast_to(np.asarray(v, np.float32).reshape(1, -1), (128, np.asarray(v).size)))


_NC_CACHE = {}


def kernel(x, w_in, lambda_q1, lambda_k1, lambda_q2, lambda_k2, subln_g, gmlp_ln_g, gmlp_ln_b,
           w_spatial, b_spatial, w_out, ln1_g, ln1_b, w_gate, w_up, conv_w, conv_b, w_down,
           ln2_g, ln2_b):
    f = lambda a: np.asarray(a, dtype=np.float32)
    x = f(x)
    w_in = f(w_in)[0]
    w_out = f(w_out)[0]
    w_gate = f(w_gate)[0]
    w_up = f(w_up)[0]
    w_down = f(w_down)[0]
    conv_w = f(conv_w)[0]
    conv_b = f(conv_b)[0]
    w_spatial = f(w_spatial)[0]
    b_spatial = f(b_spatial)[0]

    perm = np.arange(512).reshape(8, 64)
    perm = np.concatenate([perm[:, 32:], perm[:, :32]], axis=1).reshape(-1)
    Wq = w_in[:, 0:512]
    Wk = w_in[:, 512:1024]
    Wv = w_in[:, 1024:1536]
    Wu = w_in[:, 1536:2048]
    Wvg = w_in[:, 2048:2560]
    shared = {
        "wk": _chunk_rows(np.concatenate([Wk, Wk[:, perm]], axis=1)).reshape(128, -1),
        "wq": _chunk_rows(np.concatenate([Wq, Wq[:, perm]], axis=1)).reshape(128, -1),
        "wv": _chunk_rows(Wv).reshape(128, -1),
        "wvg": _chunk_rows(Wvg).reshape(128, -1),
        "wu": _chunk_rows(Wu).reshape(128, -1),
        "wo": _chunk_rows(w_out).reshape(128, -1),
        "wgu": np.ascontiguousarray(np.stack([w_gate, w_up], axis=0).reshape(2, 8, 128, NFC, 128)
                                    .transpose(3, 2, 0, 1, 4)).reshape(NFC, 128, -1),
        "wdn": np.ascontiguousarray(w_down.reshape(NFC, 128, 1024)),
        "lntab": np.concatenate([_bcast(f(ln1_g)[0]), _bcast(f(ln1_b)[0]), _bcast(f(ln2_g)[0]), _bcast(f(ln2_b)[0])], axis=1),
        "gmtab": np.concatenate([_bcast(f(gmlp_ln_g)[0]), _bcast(f(gmlp_ln_b)[0])], axis=1),
        "bstab": _bcast(b_spatial.reshape(-1)),
        "wsT": np.ascontiguousarray(w_spatial.transpose(2, 0, 1)).reshape(128, -1),
        "convtab": np.ascontiguousarray(np.concatenate([conv_w, conv_b[None, :]], axis=0).reshape(4, NFC, 128)
                                        .transpose(2, 1, 0)).reshape(128, -1),
        "lamtab": np.concatenate([_bcast(f(lambda_q1)[0]), _bcast(f(lambda_k1)[0]),
                                  _bcast(f(lambda_q2)[0]), _bcast(f(lambda_k2)[0])], axis=1),
        "subg": np.ascontiguousarray(f(subln_g)[0].reshape(128, 1)),
    }
    shared = {k: np.ascontiguousarray(v, dtype=np.float32) for k, v in shared.items()}

    per_role = {}
    for r in range(2):
        ktok, flg = _core_tokens(r)
        per_role[r] = dict(
            ktok=ktok,
            rope=_rope_tables(ktok),
            kbias=np.ascontiguousarray(np.broadcast_to(_kbias_table(ktok)[None, :], (128, NKBIAS))).astype(np.float32),
            flags=np.ascontiguousarray(np.broadcast_to(np.asarray(flg, np.float32)[None, :], (128, 2))),
            qtok=np.concatenate([ktok[0:4096], ktok[8192:8448]]),
        )

    in_maps = []
    for c in range(8):
        b, r = c // 2, c % 2
        pr = per_role[r]
        xb = x[b]
        xk = xb[pr["ktok"]]
        xT = np.ascontiguousarray(xk.reshape(NKB, 256, 8, 128).transpose(0, 3, 2, 1)).reshape(NKB, 128, 8 * 256)
        m = dict(shared)
        m["xT"] = xT
        m["xq"] = np.ascontiguousarray(xb[pr["qtok"]])
        m["rope"] = pr["rope"]
        m["kbias"] = pr["kbias"]
        m["flags"] = pr["flags"]
        in_maps.append(m)

    if "nc" not in _NC_CACHE:
        _NC_CACHE["nc"] = build_nc()
    nc = _NC_CACHE["nc"]
    res = run_bass_kernel_spmd(nc, in_maps, core_ids=list(range(8)))
    out = np.empty((NB, S, D), np.float32)
    for c in range(8):
        b, r = c // 2, c % 2
        own = per_role[r]["ktok"][0:4096]
        out[b, own] = res.results[c]["y"]
    return out
```
